# Optimizing a Trainium2 kernel written in Bass

```python
import math
import jax, jax.numpy as jnp
from jax import lax
import numpy as np

D_MODEL = 1024
BATCH = 8
SEQ = 4096
DEPTH = 1

MIX_WIDTH = D_MODEL
HG_HEADS = 4
HG_DK = 128
HG_DV = 128
HG_KEY_WIDTH = HG_HEADS * HG_DK
HG_VAL_WIDTH = HG_HEADS * HG_DV
HG_CHUNK = 64
ATT_HEADS = 8
ATT_KV_HEADS = 2
ATT_GROUP = ATT_HEADS // ATT_KV_HEADS
ATT_HD = 64
ATT_Q_WIDTH = ATT_HEADS * ATT_HD
ATT_KV_WIDTH = ATT_KV_HEADS * ATT_HD
WINDOW = 128
ATT_BLOCK = 128
HG_COLS = 2 * HG_KEY_WIDTH + 2 * HG_VAL_WIDTH
ATT_COLS = ATT_Q_WIDTH + 2 * ATT_KV_WIDTH
IN_COLS = HG_COLS + ATT_COLS
D_FF = 2816
CONV_W = 3
EPS = 1e-6

kernel_name = "hymba_hgrn2_swa_sink_convffn"


def rmsnorm(x, w, eps=EPS):
    xf = x.astype(jnp.float32)
    inv = lax.rsqrt(jnp.mean(xf * xf, axis=-1, keepdims=True) + eps)
    return (xf * inv * w.astype(jnp.float32)).astype(x.dtype)


def hgrn2_chunkwise(q, k, v, log_f):
    B, S, H, DK = q.shape
    DV = v.shape[-1]
    n = S // HG_CHUNK

    def to_chunks(a):
        return a.astype(jnp.float32).reshape(B, n, HG_CHUNK, H, a.shape[-1]).transpose(1, 0, 3, 2, 4)

    qc, kc, vc, gc = to_chunks(q), to_chunks(k), to_chunks(v), to_chunks(log_f)
    causal = jnp.tril(jnp.ones((HG_CHUNK, HG_CHUNK), dtype=bool))[:, :, None]

    def step(state, inp):
        qi, ki, vi, gi = inp
        b = jnp.cumsum(gi, axis=2)
        rel = b[:, :, :, None, :] - b[:, :, None, :, :]
        decay = jnp.exp(jnp.where(causal, rel, -jnp.inf))
        scores = jnp.einsum('bhtk,bhsk,bhtsk->bhts', qi, ki, decay)
        intra = jnp.einsum('bhts,bhsv->bhtv', scores, vi)
        inter = jnp.einsum('bhtk,bhkv->bhtv', qi * jnp.exp(b), state)
        b_last = b[:, :, -1:, :]
        new_state = state * jnp.exp(b_last)[:, :, 0, :, None] + jnp.einsum(
            'bhsk,bhsv->bhkv', ki * jnp.exp(b_last - b), vi)
        return new_state, intra + inter

    state0 = jnp.zeros((B, H, DK, DV), jnp.float32)
    _, out = lax.scan(step, state0, (qc, kc, vc, gc))
    return out.transpose(1, 0, 3, 2, 4).reshape(B, S, H, DV)


def sliding_window_attention_with_sinks(q, k, v, sinks):
    B, S = q.shape[0], q.shape[1]
    nb = S // ATT_BLOCK
    scale = 1.0 / math.sqrt(ATT_HD)
    qb = q.astype(jnp.float32).reshape(B, nb, ATT_BLOCK, ATT_KV_HEADS, ATT_GROUP, ATT_HD)

    def band_keys(a):
        ap = jnp.pad(a.astype(jnp.float32), ((0, 0), (ATT_BLOCK, 0), (0, 0), (0, 0)))
        ap = ap.reshape(B, nb + 1, ATT_BLOCK, ATT_KV_HEADS, ATT_HD)
        return jnp.concatenate([ap[:, :-1], ap[:, 1:]], axis=2)

    kb, vb = band_keys(k), band_keys(v)
    scores = jnp.einsum('bnqhgd,bnkhd->bnhgqk', qb, kb) * scale
    qi = jnp.arange(ATT_BLOCK)[:, None]
    kj = jnp.arange(2 * ATT_BLOCK)[None, :]
    dist = qi + ATT_BLOCK - kj
    band = (dist >= 0) & (dist < WINDOW)
    key_pos = jnp.arange(nb)[:, None] * ATT_BLOCK + jnp.arange(2 * ATT_BLOCK)[None, :] - ATT_BLOCK
    mask = band[None] & (key_pos >= 0)[:, None, :]
    scores = jnp.where(mask[None, :, None, None], scores, -jnp.inf)
    sink = sinks.astype(jnp.float32).reshape(ATT_KV_HEADS, ATT_GROUP)[None, None, :, :, None, None]
    m = jnp.maximum(jnp.max(scores, axis=-1, keepdims=True), sink)
    p = jnp.exp(scores - m)
    denom = jnp.sum(p, axis=-1, keepdims=True) + jnp.exp(sink - m)
    out = jnp.einsum('bnhgqk,bnkhd->bnqhgd', p / denom, vb)
    return out.reshape(B, S, ATT_HEADS * ATT_HD)


def causal_depthwise_conv(a, w, b):
    C = a.shape[-1]
    y = lax.conv_general_dilated(
        a, w[:, None, :].astype(a.dtype), window_strides=(1,), padding=[(CONV_W - 1, 0)],
        dimension_numbers=('NWC', 'WIO', 'NWC'), feature_group_count=C)
    return y + b.astype(a.dtype)


def setup_inputs(seed: int = 0) -> dict:
    key = jax.random.key(seed)
    ks = jax.random.split(key, 16)
    f32 = jnp.float32
    nrm = lambda k, shape, s: jax.random.normal(k, shape, f32) * s
    return {
        "x": jax.random.normal(ks[0], (BATCH, SEQ, D_MODEL), f32),
        "norm_mix_w": 1.0 + nrm(ks[1], (DEPTH, D_MODEL), 0.02),
        "w_in": nrm(ks[2], (DEPTH, D_MODEL, IN_COLS), D_MODEL ** -0.5),
        "b_attn": nrm(ks[3], (DEPTH, ATT_COLS), 0.02),
        "lb_logits": nrm(ks[4], (DEPTH + 1, HG_KEY_WIDTH), 0.1),
        "hg_norm_w": 1.0 + nrm(ks[5], (DEPTH, HG_DV), 0.02),
        "sinks": nrm(ks[6], (DEPTH, ATT_HEADS), 0.5),
        "w_out": nrm(ks[7], (DEPTH, MIX_WIDTH, D_MODEL), MIX_WIDTH ** -0.5),
        "norm_ffn_w": 1.0 + nrm(ks[8], (DEPTH, D_MODEL), 0.02),
        "w_gate": nrm(ks[9], (DEPTH, D_MODEL, D_FF), D_MODEL ** -0.5),
        "w_up": nrm(ks[10], (DEPTH, D_MODEL, D_FF), D_MODEL ** -0.5),
        "conv_w": nrm(ks[11], (DEPTH, CONV_W, D_FF), CONV_W ** -0.5),
        "conv_b": nrm(ks[12], (DEPTH, D_FF), 0.02),
        "w_down": nrm(ks[13], (DEPTH, D_FF, D_MODEL), D_FF ** -0.5),
        "final_norm_w": 1.0 + nrm(ks[14], (D_MODEL,), 0.02),
    }


def reference(x, norm_mix_w, w_in, b_attn, lb_logits, hg_norm_w, sinks, w_out,
              norm_ffn_w, w_gate, w_up, conv_w, conv_b, w_down, final_norm_w):
    B, S, _ = x.shape
    lb_all = jnp.cumsum(jax.nn.softmax(lb_logits.astype(jnp.float32), axis=0), axis=0)[:DEPTH]
    h = x
    for l in range(DEPTH):
        u = rmsnorm(h, norm_mix_w[l])
        proj = u @ w_in[l]
        hq, hf, hi, hg, att = jnp.split(
            proj, [HG_KEY_WIDTH, 2 * HG_KEY_WIDTH, 2 * HG_KEY_WIDTH + HG_VAL_WIDTH, HG_COLS], axis=-1)
        lb = lb_all[l]
        f = lb + (1.0 - lb) * jax.nn.sigmoid(hf.astype(jnp.float32))
        log_f = jnp.log(f).reshape(B, S, HG_HEADS, HG_DK)
        k_hg = (1.0 - f).reshape(B, S, HG_HEADS, HG_DK)
        q_hg = hq.astype(jnp.float32).reshape(B, S, HG_HEADS, HG_DK) * (HG_DK ** -0.5)
        v_hg = hi.reshape(B, S, HG_HEADS, HG_DV)
        o_hg = hgrn2_chunkwise(q_hg, k_hg, v_hg, log_f)
        o_hg = rmsnorm(o_hg, hg_norm_w[l]).reshape(B, S, HG_VAL_WIDTH)
        o_hg = (o_hg * jax.nn.silu(hg.astype(jnp.float32))).astype(h.dtype)
        att = att + b_attn[l]
        aq, ak, av = jnp.split(att, [ATT_Q_WIDTH, ATT_Q_WIDTH + ATT_KV_WIDTH], axis=-1)
        o_att = sliding_window_attention_with_sinks(
            aq.reshape(B, S, ATT_HEADS, ATT_HD),
            ak.reshape(B, S, ATT_KV_HEADS, ATT_HD),
            av.reshape(B, S, ATT_KV_HEADS, ATT_HD),
            sinks[l]).astype(h.dtype)
        mix = jnp.concatenate([o_hg, o_att], axis=-1)
        h = h + mix @ w_out[l]
        v = rmsnorm(h, norm_ffn_w[l])
        gate = causal_depthwise_conv(v @ w_gate[l], conv_w[l], conv_b[l])
        h = h + (jax.nn.silu(gate) * (v @ w_up[l])) @ w_down[l]
    return rmsnorm(h, final_norm_w)
```

```python
import os
from contextlib import ExitStack

import numpy as np
import concourse.bass as bass
import concourse.mybir as mybir
from concourse.bass_utils import run_bass_kernel_spmd

F32 = mybir.dt.float32
BF16 = mybir.dt.bfloat16
ALU = mybir.AluOpType
AF = mybir.ActivationFunctionType

P = 128
D = 1024
KC = 8
T = 512
NSUB = 4
SEQ = 4096
DFF = 2816
NFC = 22
INC = 2816
EPS = 1e-6
NW = 6
WSLOT = 2048

ENGS = ["pe", "act", "dve", "pool", "sp"]
SAME_ENG_WINDOW = 4


class Sched:
    def __init__(self):
        self.ops = {e: [] for e in ENGS}
        self.last_w = {}
        self.readers = {}
        self.dma_cnt = {}
        self.seen = {c: {} for c in ENGS}
        self.seen_dma = {c: {} for c in ENGS}
        self.final_dma = []

    def op(self, eng, fn, reads=(), writes=(), dma=None):
        idx = len(self.ops[eng])
        deps = []
        for b in reads:
            w = self.last_w.get(b)
            if w is not None:
                deps.append(w)
        for b in writes:
            w = self.last_w.get(b)
            if w is not None:
                deps.append(w)
            deps.extend(self.readers.get(b, ()))
        rec = {"eng": eng, "idx": idx, "fn": fn, "waits": [], "inc": False, "dma": dma, "dval": None}
        if dma is not None:
            self.dma_cnt[dma] = self.dma_cnt.get(dma, 0) + 16
            rec["dval"] = self.dma_cnt[dma]
        comp_w = {}
        dma_w = {}
        for d in deps:
            if d["dma"] is not None:
                t = d["dma"]
                if d["dval"] > self.seen_dma[eng].get(t, 0):
                    dma_w[t] = max(dma_w.get(t, 0), d["dval"])
            else:
                p = d["eng"]
                if p == eng:
                    if eng == "pe":
                        continue
                    if d["idx"] < idx - SAME_ENG_WINDOW:
                        continue
                if d["idx"] > self.seen[eng].get(p, -1):
                    if p not in comp_w or d["idx"] > comp_w[p]["idx"]:
                        comp_w[p] = d
        for t, v in dma_w.items():
            self.seen_dma[eng][t] = v
            rec["waits"].append(("dma", t, v))
        for p, d in comp_w.items():
            self.seen[eng][p] = d["idx"]
            d["inc"] = True
            rec["waits"].append(("eng", p, d))
        self.ops[eng].append(rec)
        for b in reads:
            self.readers.setdefault(b, []).append(rec)
        for b in writes:
            self.last_w[b] = rec
            self.readers[b] = []
        return rec

    def wait_all(self, eng, recs):
        self.final_dma.append((eng, recs))

    def emit(self, nc):
        with ExitStack() as es:
            esem = {e: es.enter_context(nc.semaphore("s_" + e)) for e in ENGS}
            dsem = {t: es.enter_context(nc.semaphore("d_" + t)) for t in self.dma_cnt}
            for e in ENGS:
                c = 0
                for r in self.ops[e]:
                    if r["dma"] is None and r["inc"]:
                        c += 1
                        r["cnt"] = c
            block = es.enter_context(nc.Block())

            def run(e, eng):
                for r in self.ops[e]:
                    for w in r["waits"]:
                        if w[0] == "dma":
                            eng.wait_ge(dsem[w[1]], w[2])
                        else:
                            eng.wait_ge(esem[w[1]], w[2]["cnt"])
                    ins = r["fn"](eng)
                    if r["dma"] is not None:
                        ins.then_inc(dsem[r["dma"]], 16)
                    elif r["inc"]:
                        ins.then_inc(esem[e], 1)
                for (fe, recs) in self.final_dma:
                    if fe == e:
                        for d in recs:
                            eng.wait_ge(dsem[d["dma"]], d["dval"])

            @block.tensor
            def _(eng):
                run("pe", eng)

            @block.scalar
            def _(eng):
                run("act", eng)

            @block.vector
            def _(eng):
                run("dve", eng)

            @block.gpsimd
            def _(eng):
                run("pool", eng)

            @block.sync
            def _(eng):
                run("sp", eng)


C_LBL = 0
C_HGW = 8
C_SINK = 9
C_BQ = 17
C_BK = 21
C_CW = 22
C_CB = 88
C_BV = 110
NCST = 238


def _tiles(specs):
    out, o = [], 0
    for (a, b) in specs:
        out.append((o, a, b))
        o += a * b
    return out


W_IN_FM = [512, 1024, 1536, 0]
W_IN_HI = 2048
W_IN_AKAV = 2560
W_IN_TILES = _tiles([(KC, 256)] * 8 + [(4, 512)] * 2 + [(KC, 256)])
W_OUT_TILES = _tiles([(4, 512)] * 4)
W_GU_TILES = _tiles([(KC, 256)] * 11)
W_DOWN_SPECS = [(nh, fg, 4 if fg < 5 else 2) for nh in range(2) for fg in range(6)]
W_DOWN_TILES = _tiles([(a, 512) for (_, _, a) in W_DOWN_SPECS])
DOWN_TILES = [(4, 0), (4, 4), (4, 8), (4, 12), (4, 16), (2, 20)]


def build_nc(nst=SEQ // T, dbg=None):
    nc = bass.Bass("TRN2", target_bir_lowering=False)
    ntok = nst * T
    x_d = nc.dram_tensor("x", [ntok, D], F32, kind="ExternalInput").ap()
    win_d = nc.dram_tensor("w_in", [P, KC * INC], F32, kind="ExternalInput").ap()
    wout_d = nc.dram_tensor("w_out", [P, KC * D], F32, kind="ExternalInput").ap()
    wg_d = nc.dram_tensor("w_gate", [P, KC * DFF], F32, kind="ExternalInput").ap()
    wu_d = nc.dram_tensor("w_up", [P, KC * DFF], F32, kind="ExternalInput").ap()
    wd_d = nc.dram_tensor("w_down", [P, NFC * D], F32, kind="ExternalInput").ap()
    nw_d = nc.dram_tensor("nw3", [P, 3, D], F32, kind="ExternalInput").ap()
    cst_d = nc.dram_tensor("cst", [P, NCST], F32, kind="ExternalInput").ap()
    out_d = nc.dram_tensor("out", [ntok, D], F32, kind="ExternalOutput").ap()
    dbg_d = {}
    if dbg:
        for name, (shape, dt) in dbg.items():
            dbg_d[name] = nc.dram_tensor("dbg_" + name, shape, dt, kind="ExternalOutput").ap()

    TPS = len(W_IN_TILES) + len(W_OUT_TILES) + 2 * len(W_GU_TILES) + len(W_DOWN_TILES)
    scr_cols = KC * INC + KC * D + 2 * KC * DFF + NFC * D
    scr_d = nc.dram_tensor("wscr", [P, scr_cols], BF16, kind="Internal").ap()
    S = Sched()
    with ExitStack() as es:
        def sb(name, shape, dt):
            return es.enter_context(nc.sbuf_tensor(name, shape, dt))

        xt = [sb(f"xt{i}", [P, NSUB, D], F32) for i in range(2)]
        ubf = [sb(f"ubf{i}", [P, D], BF16) for i in range(2)]
        uT = sb("uT", [P, KC, T], BF16)
        vT = sb("vT", [P, KC, T], BF16)
        wsl = [sb(f"wsl{i}", [P, WSLOT], BF16) for i in range(NW)]
        nw3 = sb("nw3s", [P, 3, D], F32)
        cst = sb("csts", [P, NCST], F32)
        stat = sb("stat", [P, 64], F32)
        cns = sb("cns", [P, 40], F32)
        ident = sb("ident", [P, P], BF16)
        ones = sb("ones", [P, P], BF16)
        scanmask = sb("scanmask", [P, T], F32)
        hmask = sb("hmask", [P, 4, P], F32)
        amask = sb("amask", [P, 2, 4, P], BF16)
        fT = [[sb(f"f{n}{i}", [P, T], F32) for n in "ABCD"] for i in range(2)]
        eb = sb("eb", [P, 4, T], F32)
        qtT = sb("qtT", [P, 4, T], BF16)
        ktT = sb("ktT", [P, 4, T], BF16)
        ktok = sb("ktok", [P, NSUB, 512], BF16)
        vtok = sb("vtok", [P, NSUB, 512], BF16)
        gsil = sb("gsil", [P, 4, T], F32)
        Sst = sb("Sst", [P, 4, P], F32)
        Sbf = [sb(f"Sbf{i}", [P, 4, P], BF16) for i in range(3)]
        stmp = sb("stmp", [P, 4, P], F32)
        amaskf = stmp
        sTm = [sb(f"sTm{i}", [P, 4, P], BF16) for i in range(2)]
        sq = sb("sq", [P, 512], BF16)
        rs = sb("rs", [P, 512], F32)
        t1 = sb("t1", [P, 512], F32)
        aqT = sb("aqT", [P, 4, T], BF16)
        akT = sb("akT", [P, 5 * P], BF16)
        vaug = sb("vaug", [P, 5, 2, 65], BF16)
        PT = [sb(f"PT{i}", [P, 512], BF16) for i in range(4)]
        den = sb("den", [P, 8], F32)
        oatt = sb("oatt", [P, 512], BF16)
        mixT = sb("mixT", [P, KC, T], BF16)
        Gb = [sb(f"Gb{i}", [P, T + 2], F32) for i in range(2)]
        cA = [sb(f"cA{i}", [P, T], F32) for i in range(2)]
        slb = [rs, t1]
        gA = cA
        carry = sb("carry", [P, NFC, 2], F32)
        actT = sb("actT", [P, NFC, T], BF16)
        ps = [es.enter_context(nc.psum_tensor(f"ps{i}", [P, 512], F32)) for i in range(8)]

        def psb(i):
            return ps[i][:].bitcast(BF16)

        lb = cns[:, 0:4]
        oml = cns[:, 4:8]
        noml = cns[:, 8:12]
        esink = cns[:, 12:20]
        bq8 = cns[:, 20:24]
        lnoml = cns[:, 28:32]
        nhalf = cns[:, 32:33]
        hgw = cst[:, C_HGW:C_HGW + 1]
        bk = cst[:, C_BK:C_BK + 1]
        convw = cst[:, C_CW:C_CW + 66].rearrange("p (c j) -> p c j", j=3)
        convb = cst[:, C_CB:C_CB + NFC]
        bv = cst[:, C_BV:C_BV + 128]

        wseq = []

        def _add(mat, d, tiles, reps=1):
            for r in range(reps):
                pass
            for ti, (o, a, b) in enumerate(tiles):
                wseq.append((d[:, o:o + a * b], a, b, (mat, ti)))

        def _add_down():
            nt = len(DOWN_TILES)
            for nh in range(2):
                for pair in range(2):
                    for ti in range(nt):
                        o, a, b = W_DOWN_TILES[nh * nt + ti]
                        wseq.append((wd_d[:, o:o + a * b], a, b, ("d", nh * nt + ti)))

        for st in range(nst):
            _add("i", win_d, W_IN_TILES)
            if st > 0:
                _add_down()
            _add("o", wout_d, W_OUT_TILES)
            for g, (o, a, b) in enumerate(W_GU_TILES):
                wseq.append((wg_d[:, o:o + a * b], a, b, ("g", g)))
                wseq.append((wu_d[:, o:o + a * b], a, b, ("u", g)))
        _add_down()
        wstate = {"loaded": 0, "used": 0, "released": 0}
        scr_off = {}
        scr_next = [0]
        last_use = {}
        for j, (_, a, b, sid) in enumerate(wseq):
            last_use[sid] = j

        def w_pump():
            while wstate["loaded"] < min(len(wseq), wstate["released"] + NW):
                j = wstate["loaded"]
                src, a, b, sid = wseq[j]
                slot = j % NW
                dst = wsl[slot][:, 0:a * b]
                if sid not in scr_off:
                    scr_off[sid] = scr_next[0]
                    scr_next[0] += a * b
                    scr = scr_d[:, scr_off[sid]:scr_off[sid] + a * b]
                    S.op("pool", (lambda e, dst=dst, src=src: e.dma_start(out=dst, in_=src)), [], [f"w{slot}"], dma=f"w{slot}")
                    if last_use[sid] > j:
                        S.op("sp", (lambda e, dst=dst, scr=scr: e.dma_start(out=scr, in_=dst)), [f"w{slot}"], [f"scr{sid}"], dma=f"wb{slot}")
                else:
                    scr = scr_d[:, scr_off[sid]:scr_off[sid] + a * b]
                    S.op("pool", (lambda e, dst=dst, scr=scr: e.dma_start(out=dst, in_=scr)), [f"scr{sid}"], [f"w{slot}"], dma=f"w{slot}")
                wstate["loaded"] += 1

        def w_next(hold=False):
            if not hold:
                w_rel(wstate["used"] - wstate["released"])
            i = wstate["used"]
            assert i < wstate["loaded"], "weight tile not yet loaded: too many tiles held"
            wstate["used"] += 1
            slot = i % NW
            _, a, b, _sid = wseq[i]
            return wsl[slot][:, 0:a * b].rearrange("p (a b) -> p a b", b=b), f"w{slot}"

        def w_rel(n=1):
            wstate["released"] += n
            w_pump()

        S.op("sp", lambda e: e.dma_start(out=cst[:], in_=cst_d), [], ["cst"], dma="c0")
        S.op("sp", lambda e: e.dma_start(out=nw3[:], in_=nw_d), [], ["nw3"], dma="c1")
        xrecs = {}

        def x_load(st):
            b = st % 2
            src = x_d[st * T:(st + 1) * T, :].rearrange("(s p) d -> p s d", p=P)
            xrecs[st] = S.op("sp", lambda e: e.dma_start(out=xt[b][:], in_=src), [],
                             [f"xt{b}s{s}" for s in range(NSUB)], dma=f"x{b}")

        x_load(0)
        w_pump()

        tmp4 = cns[:, 24:28]
        S.op("dve", lambda e: e.tensor_tensor(out=tmp4, in0=cst[:, C_LBL + 4:C_LBL + 8], in1=cst[:, C_LBL:C_LBL + 4], op=ALU.subtract), ["cst"], ["tmp4"])
        S.op("act", lambda e: e.activation(out=tmp4, in_=tmp4, func=AF.Exp), ["tmp4"], ["tmp4"])
        S.op("dve", lambda e: e.tensor_scalar_add(out=lb, in0=tmp4, scalar1=1.0), ["tmp4"], ["lb"])
        S.op("dve", lambda e: e.reciprocal(out=lb, in_=lb), ["lb"], ["lb"])
        S.op("dve", lambda e: e.tensor_tensor(out=oml, in0=tmp4, in1=lb, op=ALU.mult), ["tmp4", "lb"], ["oml"])
        S.op("dve", lambda e: e.tensor_scalar_mul(out=noml, in0=oml, scalar1=-1.0), ["oml"], ["noml"])
        S.op("act", lambda e: e.activation(out=lnoml, in_=oml, func=AF.Ln), ["oml"], ["lnoml"])
        S.op("act", lambda e: e.activation(out=esink, in_=cst[:, C_SINK:C_SINK + 8], func=AF.Exp), ["cst"], ["esink"])
        S.op("dve", lambda e: e.tensor_scalar_mul(out=bq8, in0=cst[:, C_BQ:C_BQ + 4], scalar1=0.125), ["cst"], ["bq8"])
        S.op("pool", lambda e: e.memset(nhalf, -0.5), [], ["nhalf"])
        S.op("pool", lambda e: e.memset(ident[:], 1.0), [], ["ident"])
        S.op("pool", lambda e: e.affine_select(out=ident[:], in_=ident[:], pattern=[[-1, P]], compare_op=ALU.is_equal, fill=0.0, base=0, channel_multiplier=1), ["ident"], ["ident"])
        S.op("pool", lambda e: e.memset(ones[:], 1.0), [], ["ones"])
        S.op("pool", lambda e: e.memset(scanmask[:], 1.0), [], ["scanmask"])
        smv = scanmask[:].rearrange("p (c t) -> p c t", t=64)
        S.op("pool", lambda e: e.memset(smv[:, :, 0:1], 0.0), ["scanmask"], ["scanmask"])
        S.op("pool", lambda e: e.memset(hmask[:], 1.0), [], ["hmask"])
        S.op("pool", lambda e: e.affine_select(out=hmask[:], in_=hmask[:], pattern=[[0, 4], [1, P]], compare_op=ALU.is_ge, fill=0.0, base=0, channel_multiplier=-1), ["hmask"], ["hmask"])
        S.op("pool", lambda e: e.memset(hmask[0:64, :, 64:128], 0.0), ["hmask"], ["hmask"])
        S.op("pool", lambda e: e.memset(amaskf[:], 1.0), [], ["stmp"])
        S.op("pool", lambda e: e.affine_select(out=amaskf[:], in_=amaskf[:], pattern=[[0, 4], [-1, P]], compare_op=ALU.is_ge, fill=0.0, base=-1, channel_multiplier=1), ["stmp"], ["stmp"])
        S.op("dve", lambda e: e.tensor_copy(out=amask[:, 0], in_=amaskf[:]), ["stmp"], ["amask"])
        S.op("pool", lambda e: e.memset(amaskf[:], 1.0), ["stmp"], ["stmp"])
        S.op("pool", lambda e: e.affine_select(out=amaskf[:], in_=amaskf[:], pattern=[[0, 4], [1, P]], compare_op=ALU.is_ge, fill=0.0, base=0, channel_multiplier=-1), ["stmp"], ["stmp"])
        S.op("dve", lambda e: e.tensor_copy(out=amask[:, 1], in_=amaskf[:]), ["stmp"], ["amask"])
        S.op("pool", lambda e: e.memset(Sst[:], 0.0), [], ["Sst"])
        S.op("pool", lambda e: e.memset(Sbf[0][:], 0.0), [], ["Sbf0"])
        S.op("pool", lambda e: e.memset(carry[:], 0.0), [], ["carry"])
        S.op("pool", lambda e: e.memset(vaug[:], 1.0), [], ["vaug"])
        S.op("pool", lambda e: e.memset(akT[:, 0:P], 0.0), [], ["akT0"])

        stat_ctr = [0]

        def new_stat():
            i = stat_ctr[0] % 16
            stat_ctr[0] += 1
            return stat[:, 4 * i:4 * i + 1], stat[:, 4 * i + 1:4 * i + 2], f"stat{i}"

        tr_bank = [0]
        dense_bank = [0]

        def next_dense_bank():
            b = [0, 1, 4, 6, 2, 5, 7, 3][dense_bank[0] % 8]
            dense_bank[0] += 1
            return b

        def dump(name, ap, keys):
            if dbg and name in dbg_d:
                S.op("sp", lambda e: e.dma_start(out=dbg_d[name], in_=ap), keys, [], dma="dbg_" + name)

        out_recs = []
        SC_Q = 128 ** -0.5
        UT1 = [f"uT{s}" for s in range(NSUB)]
        UT2 = [f"vT{s}" for s in range(NSUB)]

        def norm_a(xb, nwi, s):
            ssq, rstd, sk = new_stat()
            ub = ubf[s % 2]
            ubk = f"ubf{s % 2}"
            xin = xt[xb][:, s, :]
            S.op("act", lambda e: e.activation(out=ub[:], in_=xin, func=AF.Square, accum_out=ssq), [f"xt{xb}s{s}"], [ubk, sk])
            S.op("pool", lambda e: e.tensor_scalar(out=rstd, in0=ssq, scalar1=1.0 / D, scalar2=EPS, op0=ALU.mult, op1=ALU.add), [sk], [sk])
            S.op("pool", lambda e: e.tensor_tensor(out=rstd, in0=rstd, in1=nhalf, op=ALU.pow), [sk, "nhalf"], [sk])
            S.op("dve", lambda e: e.scalar_tensor_tensor(out=ub[:], in0=xin, scalar=rstd, in1=nw3[:, nwi, :], op0=ALU.mult, op1=ALU.mult),
                 [f"xt{xb}s{s}", sk, "nw3"], [ubk])

        def norm_b(s, dstT, dkeys):
            ub = ubf[s % 2]
            ubk = f"ubf{s % 2}"
            bank = tr_bank[0] % 2
            tr_bank[0] += 1
            pv = psb(bank).rearrange("p (c t) -> p c t", t=P)
            for c in range(KC):
                S.op("pe", lambda e, c=c: e.transpose(out=pv[:, c, :], in_=ub[:, c * P:(c + 1) * P], identity=ident[:]), [ubk, "ident"], [f"ps{bank}"])
            dst = dstT[:, :, s * P:(s + 1) * P]
            S.op("act", lambda e: e.activation(out=dst, in_=pv, func=AF.Copy), [f"ps{bank}"], [dkeys[s]])

        def norm_gen(xb, nwi, dstT, dkeys):
            for s in range(NSUB):
                norm_a(xb, nwi, s)
                yield
                yield
                norm_b(s, dstT, dkeys)
                yield

        def norm_T(xb, nwi, dstT, dkeys):
            for _ in norm_gen(xb, nwi, dstT, dkeys):
                pass

        def interleave(gens):
            active = list(gens)
            while active:
                nxt_active = []
                for g in active:
                    try:
                        next(g)
                        nxt_active.append(g)
                    except StopIteration:
                        pass
                active = nxt_active

        def fm_chunk(w, wk, m, bank, srcT, skeys):
            for kc in range(KC):
                S.op("pe", lambda e, kc=kc: e.matmul(ps[bank][:, 0:T], lhsT=w[:, kc, m * P:(m + 1) * P], rhs=srcT[:, kc, :], start=(kc == 0), stop=(kc == KC - 1)),
                     [wk] + skeys, [f"ps{bank}"])

        def evac_f(h, bank):
            fA, fB, fC, fD = fT[h % 2]
            kA, kB, kC_, kD = [f"f{n}{h % 2}" for n in "ABCD"]
            pk = f"ps{bank}"
            S.op("act", lambda e: e.activation(out=fA[:], in_=ps[bank][:], func=AF.Exp, scale=-1.0), [pk], [kA])
            S.op("act", lambda e: e.activation(out=fB[:], in_=fA[:], func=AF.Ln, bias=1.0), [kA], [kB])
            S.op("act", lambda e: e.activation(out=fC[:], in_=fA[:], func=AF.Ln, scale=lb[:, h:h + 1], bias=1.0), [kA, "lb"], [kC_])
            S.op("dve", lambda e: e.tensor_tensor(out=fC[:], in0=fC[:], in1=fB[:], op=ALU.subtract), [kC_, kB], [kC_])
            S.op("dve", lambda e: e.tensor_tensor_scan(out=fD[:], data0=scanmask[:], data1=fC[:], initial=0.0, op0=ALU.mult, op1=ALU.add),
                 [kC_, "scanmask"], [kD])
            S.op("dve", lambda e: e.tensor_tensor(out=fA[:], in0=ps[bank][:], in1=fB[:], op=ALU.add), [pk, kB], [kA])
            S.op("dve", lambda e: e.tensor_tensor(out=fA[:], in0=fA[:], in1=fD[:], op=ALU.add), [kA, kD], [kA])
            S.op("act", lambda e: e.activation(out=ktT[:, h, :], in_=fA[:], func=AF.Exp, scale=-1.0, bias=lnoml[:, h:h + 1]), [kA, "lnoml"], [f"ktT{h}"])
            S.op("act", lambda e: e.activation(out=eb[:, h, :], in_=fD[:], func=AF.Exp), [kD], [f"eb{h}"])

        def evac_q(h, bank):
            S.op("dve", lambda e: e.scalar_tensor_tensor(out=qtT[:, h, :], in0=ps[bank][:], scalar=SC_Q, in1=eb[:, h, :], op0=ALU.mult, op1=ALU.mult),
                 [f"ps{bank}", f"eb{h}"], [f"qtT{h}"])

        def evac_g(h, bank):
            g = gA[h % 2]
            gk = f"cA{h % 2}"
            S.op("act", lambda e: e.activation(out=g[:], in_=ps[bank][:], func=AF.Exp, scale=-1.0), [f"ps{bank}"], [gk])
            S.op("act", lambda e: e.activation(out=g[:], in_=g[:], func=AF.Ln, bias=1.0), [gk], [gk])
            S.op("act", lambda e: e.activation(out=g[:], in_=g[:], func=AF.Exp, scale=-1.0), [gk], [gk])
            S.op("dve", lambda e: e.tensor_tensor(out=gsil[:, h, :], in0=ps[bank][:], in1=g[:], op=ALU.mult), [f"ps{bank}", gk], [f"gsil{h}"])

        def evac_aq(j, bank):
            S.op("act", lambda e: e.activation(out=aqT[:, j, :], in_=ps[bank][:], func=AF.Identity, scale=0.125, bias=bq8[:, j:j + 1]),
                 [f"ps{bank}", "bq8"], [f"aqT{j}"])

        def inproj(st):
            dense_bank[0] = 0
            for evac in (evac_f, evac_g, evac_aq, evac_q):
                for hh in range(2):
                    w, wk = w_next()
                    for m in range(2):
                        bank = next_dense_bank()
                        fm_chunk(w, wk, m, bank, uT, UT1)
                        evac(hh * 2 + m, bank)
            wh = [w_next(), w_next(hold=True)]
            for s in range(NSUB):
                bank = next_dense_bank()
                for kc in range(KC):
                    w, wk = wh[kc // 4]
                    S.op("pe", lambda e, kc=kc, s=s, bank=bank, w=w: e.matmul(ps[bank][:, 0:512], lhsT=uT[:, kc, s * P:(s + 1) * P], rhs=w[:, kc % 4, :], start=(kc == 0), stop=(kc == KC - 1)),
                         [wk, f"uT{s}"], [f"ps{bank}"])
                S.op("act", lambda e, s=s, bank=bank: e.activation(out=vtok[:, s, :], in_=ps[bank][:], func=AF.Copy), [f"ps{bank}"], [f"vtok{s}"])
            w, wk = w_next()
            bank = next_dense_bank()
            fm_chunk(w, wk, 0, bank, uT, UT1)
            S.op("act", lambda e, bank=bank: e.activation(out=akT[:, P:5 * P], in_=ps[bank][:], func=AF.Identity, bias=bk),
                 [f"ps{bank}", "cst"], [f"akT{i}" for i in range(1, 5)])
            bank = next_dense_bank()
            for s in range(NSUB):
                for kc in range(KC):
                    S.op("pe", lambda e, kc=kc, s=s, bank=bank, w=w: e.matmul(ps[bank][:, s * P:(s + 1) * P], lhsT=uT[:, kc, s * P:(s + 1) * P], rhs=w[:, kc, P:2 * P], start=(kc == 0), stop=(kc == KC - 1)),
                         [wk, f"uT{s}"], [f"ps{bank}"])
            for s in range(NSUB):
                S.op("dve", lambda e, s=s, bank=bank: e.tensor_tensor(out=vaug[:, 1 + s, :, 0:64], in0=ps[bank][:, s * P:(s + 1) * P].rearrange("p (a d) -> p a d", d=64),
                                                                      in1=bv.rearrange("p (a d) -> p a d", d=64), op=ALU.add),
                     [f"ps{bank}", "cst"], [f"vaug{1 + s}"])

        def hg_gen(st, s):
            cs = slice(s * P, (s + 1) * P)
            pv = psb(7).rearrange("p (c t) -> p c t", t=P)
            for h in range(4):
                S.op("pe", lambda e, h=h: e.transpose(out=pv[:, h, :], in_=ktT[:, h, cs], identity=ident[:]), [f"ktT{h}", "ident"], ["ps7"])
            for h in range(4):
                S.op("pe", lambda e, h=h: e.matmul(ps[6][:, h * P:(h + 1) * P], lhsT=ktT[:, h, cs], rhs=qtT[:, h, cs], start=True, stop=True),
                     [f"ktT{h}", f"qtT{h}"], ["ps6"])
            yield
            S.op("act", lambda e: e.activation(out=ktok[:, s, :].rearrange("p (h k) -> p h k", k=P), in_=pv[:, 0:4, :], func=AF.Copy), ["ps7"], [f"ktok{s}"])
            sm = sTm[s % 2]
            smk = f"sTm{s % 2}"
            S.op("dve", lambda e: e.tensor_tensor(out=sm[:], in0=ps[6][:].rearrange("p (h t) -> p h t", t=P), in1=hmask[:], op=ALU.mult), ["ps6", "hmask"], [smk])
            yield
            n0 = (st * NSUB + s) * 2
            for ci in range(2):
                pr = slice(ci * 64, (ci + 1) * 64)
                for h in range(4):
                    S.op("pe", lambda e, h=h, pr=pr: e.matmul(ps[2][:, h * P:(h + 1) * P], lhsT=ktok[pr, s, h * P:(h + 1) * P], rhs=vtok[pr, s, h * P:(h + 1) * P], start=True, stop=True),
                         [f"ktok{s}", f"vtok{s}"], ["ps2"])
                yield
                tl = s * P + ci * 64 + 63
                nxt = (n0 + ci + 1) % 3
                S.op("dve", lambda e: e.tensor_tensor(out=stmp[:], in0=ps[2][:].rearrange("p (h v) -> p h v", v=P), in1=Sst[:], op=ALU.add), ["ps2", "Sst"], ["stmp"])
                S.op("dve", lambda e, tl=tl: e.tensor_tensor(out=Sst[:], in0=stmp[:], in1=eb[:, :, tl:tl + 1].to_broadcast([P, 4, P]), op=ALU.mult),
                     ["stmp"] + [f"eb{h}" for h in range(4)], ["Sst"])
                S.op("act", lambda e, nxt=nxt: e.activation(out=Sbf[nxt][:], in_=Sst[:], func=AF.Copy), ["Sst"], [f"Sbf{nxt}"])
                yield
            for h in range(4):
                S.op("pe", lambda e, h=h: e.matmul(ps[7][:, h * P:(h + 1) * P], lhsT=vtok[:, s, h * P:(h + 1) * P], rhs=sm[:, h, :], start=True, stop=False),
                     [f"vtok{s}", smk], ["ps7"])
                for ci in range(2):
                    tcol = slice(s * P + ci * 64, s * P + ci * 64 + 64)
                    cur = (n0 + ci) % 3
                    S.op("pe", lambda e, h=h, ci=ci, tcol=tcol, cur=cur: e.matmul(ps[7][:, h * P + ci * 64:h * P + ci * 64 + 64], lhsT=Sbf[cur][:, h, :], rhs=qtT[:, h, tcol], start=False, stop=(ci == 1)),
                         [f"Sbf{cur}", f"qtT{h}"], ["ps7"])
            yield
            S.op("act", lambda e: e.activation(out=sq[:], in_=ps[7][:], func=AF.Square), ["ps7"], ["sq"])
            yield
            S.op("pe", lambda e: e.matmul(ps[6][:, 0:512], lhsT=ones[:], rhs=sq[:], start=True, stop=True), ["ones", "sq"], ["ps6"])
            yield
            S.op("act", lambda e: e.activation(out=rs[:], in_=ps[6][:], func=AF.Ln, scale=1.0 / P, bias=EPS), ["ps6"], ["rs"])
            S.op("act", lambda e: e.activation(out=rs[:], in_=rs[:], func=AF.Exp, scale=-0.5), ["rs"], ["rs"])
            yield
            S.op("dve", lambda e: e.tensor_tensor(out=t1[:], in0=ps[7][:], in1=rs[:], op=ALU.mult), ["ps7", "rs"], ["t1"])
            S.op("dve", lambda e: e.scalar_tensor_tensor(out=mixT[:, 0:4, cs], in0=t1[:].rearrange("p (h t) -> p h t", t=P), scalar=hgw, in1=gsil[:, :, cs], op0=ALU.mult, op1=ALU.mult),
                 ["t1", "cst"] + [f"gsil{h}" for h in range(4)], [f"mixT{s}a"])

        def at_gen(st, s):
            nb = st * NSUB + s
            cs = slice(s * P, (s + 1) * P)
            blks = [(1, 1 + s)] if nb == 0 else [(0, s), (1, 1 + s)]
            for kvh in range(2):
                pr = slice(kvh * 64, (kvh + 1) * 64)
                for (mi, slot) in blks:
                    bank = 4 + mi
                    S.op("pe", lambda e, slot=slot, bank=bank, pr=pr: e.matmul(ps[bank][:].rearrange("p (j q) -> p j q", q=P), lhsT=akT[pr, slot * P:(slot + 1) * P], rhs=aqT[pr, :, cs], start=True, stop=True),
                         [f"akT{slot}"] + [f"aqT{j}" for j in range(4)], [f"ps{bank}"])
                yield
                for (mi, slot) in blks:
                    bank = 4 + mi
                    pt = PT[kvh * 2 + mi]
                    ptk = f"PT{kvh * 2 + mi}"
                    S.op("act", lambda e, pt=pt, bank=bank: e.activation(out=pt[:], in_=ps[bank][:], func=AF.Exp), [f"ps{bank}"], [ptk])
                    S.op("dve", lambda e, pt=pt, mi=mi: e.tensor_tensor(out=pt[:], in0=pt[:], in1=amask[:, mi].rearrange("p h q -> p (h q)"), op=ALU.mult), [ptk, "amask"], [ptk])
                yield
                for j in range(4):
                    col = j * 65
                    for bi, (mi, slot) in enumerate(blks):
                        pt = PT[kvh * 2 + mi]
                        S.op("pe", lambda e, pt=pt, j=j, slot=slot, col=col, bi=bi, kvh=kvh: e.matmul(ps[1][:, col:col + 65], lhsT=pt[:, j * P:(j + 1) * P], rhs=vaug[:, slot, kvh, :], start=(bi == 0), stop=(bi == len(blks) - 1)),
                             [f"PT{kvh * 2 + mi}", f"vaug{slot}"], ["ps1"])
                yield
                pvv = ps[1][:, 0:260].rearrange("p (h d) -> p h d", d=65)
                dn = den[:, kvh * 4:kvh * 4 + 4]
                S.op("dve", lambda e, dn=dn, kvh=kvh: e.tensor_tensor(out=dn, in0=pvv[:, :, 64], in1=esink[:, kvh * 4:kvh * 4 + 4], op=ALU.add), ["ps1", "esink"], [f"den{kvh}"])
                S.op("dve", lambda e, dn=dn: e.reciprocal(out=dn, in_=dn), [f"den{kvh}"], [f"den{kvh}"])
                S.op("dve", lambda e, dn=dn, kvh=kvh: e.tensor_tensor(out=oatt[:, kvh * 256:(kvh + 1) * 256].rearrange("p (h d) -> p h d", d=64), in0=pvv[:, :, 0:64],
                                                             in1=dn.unsqueeze(2).to_broadcast([P, 4, 64]), op=ALU.mult),
                     ["ps1", f"den{kvh}"], [f"oatt{kvh}"])
                yield
            pv1 = psb(1).rearrange("p (c t) -> p c t", t=P)
            for c in range(4):
                S.op("pe", lambda e, c=c: e.transpose(out=pv1[:, c, :], in_=oatt[:, c * P:(c + 1) * P], identity=ident[:]), [f"oatt{c // 2}", "ident"], ["ps1"])
            yield
            S.op("act", lambda e: e.activation(out=mixT[:, 4:8, cs], in_=pv1[:, 0:4, :], func=AF.Copy), ["ps1"], [f"mixT{s}b"])

        def mixer_gen(st):
            for s in range(NSUB):
                active = [hg_gen(st, s), at_gen(st, s)]
                while active:
                    nxt_active = []
                    for g in active:
                        try:
                            next(g)
                            nxt_active.append(g)
                        except StopIteration:
                            pass
                    active = nxt_active
                    yield
            S.op("act", lambda e: e.activation(out=akT[:, 0:P], in_=akT[:, 4 * P:5 * P], func=AF.Copy), ["akT4"], ["akT0"])
            S.op("pool", lambda e: e.tensor_copy(out=vaug[:, 0], in_=vaug[:, 4]), ["vaug4"], ["vaug0"])

        def down_gen(st):
            xb = st % 2
            for nh in range(2):
                for pair in range(2):
                    for ti, (nfc, c0) in enumerate(DOWN_TILES):
                        w, wk = w_next()
                        for si in range(2):
                            s = pair * 2 + si
                            bank = (0, 3)[si]
                            for fc in range(nfc):
                                c = c0 + fc
                                S.op("pe", lambda e, fc=fc, c=c, s=s, bank=bank, w=w: e.matmul(ps[bank][:, 0:512], lhsT=actT[:, c, s * P:(s + 1) * P], rhs=w[:, fc, :], start=(c == 0), stop=(c == NFC - 1)),
                                     [wk, f"actT{c}"], [f"ps{bank}"])
                            yield
                    for si in range(2):
                        s = pair * 2 + si
                        bank = (0, 3)[si]
                        xs = xt[xb][:, s, nh * 512:(nh + 1) * 512]
                        S.op("dve", lambda e, xs=xs, bank=bank: e.tensor_tensor(out=xs, in0=ps[bank][:], in1=xs, op=ALU.add), [f"ps{bank}", f"xt{xb}s{s}"], [f"xt{xb}s{s}"])

        def delayed(gen, n):
            for _ in range(n):
                yield
            yield from gen

        def tok_proj_gen(xb, srcT, skeys_fn, nk, tiles):
            for nh in range(2):
                for ti, (nfc, c0) in enumerate(tiles):
                    w, wk = w_next()
                    for s in range(NSUB):
                        bank = 2 + s
                        for fc in range(nfc):
                            c = c0 + fc
                            S.op("pe", lambda e, fc=fc, c=c, s=s, bank=bank, w=w: e.matmul(ps[bank][:, 0:512], lhsT=srcT[:, c, s * P:(s + 1) * P], rhs=w[:, fc, :], start=(c == 0), stop=(c == nk - 1)),
                                 [wk] + skeys_fn(s, c), [f"ps{bank}"])
                        yield
                for s in range(NSUB):
                    bank = 2 + s
                    xs = xt[xb][:, s, nh * 512:(nh + 1) * 512]
                    S.op("dve", lambda e, xs=xs, bank=bank: e.tensor_tensor(out=xs, in0=ps[bank][:], in1=xs, op=ALU.add), [f"ps{bank}", f"xt{xb}s{s}"], [f"xt{xb}s{s}"])

        def outproj_norm2(st):
            xb = st % 2
            wt = [w_next()] + [w_next(hold=True) for _ in range(3)]
            for s in range(NSUB):
                for nh in range(2):
                    bank = 2 + (2 * s + nh) % 4
                    for kc in range(KC):
                        w, wk = wt[nh * 2 + kc // 4]
                        S.op("pe", lambda e, kc=kc, s=s, bank=bank, w=w: e.matmul(ps[bank][:, 0:512], lhsT=mixT[:, kc, s * P:(s + 1) * P], rhs=w[:, kc % 4, :], start=(kc == 0), stop=(kc == KC - 1)),
                             [wk, f"mixT{s}a", f"mixT{s}b"], [f"ps{bank}"])
                    xs = xt[xb][:, s, nh * 512:(nh + 1) * 512]
                    S.op("dve", lambda e, xs=xs, bank=bank: e.tensor_tensor(out=xs, in0=ps[bank][:], in1=xs, op=ALU.add), [f"ps{bank}", f"xt{xb}s{s}"], [f"xt{xb}s{s}"])
                norm_a(xb, 1, s)
                if s > 0:
                    norm_b(s - 1, vT, UT2)
            norm_b(NSUB - 1, vT, UT2)

        def ffn_gateup(st):
            wts = {}

            def bufs(c):
                par = c % 2
                return Gb[par], f"Gb{par}", cA[par], f"cA{par}", slb[par], ["rs", "t1"][par], [(2, 3), (4, 5), (6, 7)][c % 3]

            def stage_pe(c):
                g, m = c // 2, c % 2
                if m == 0:
                    wts[g] = (w_next(), w_next(hold=True))
                (wg_t, wgk), (wu_t, wuk) = wts[g]
                bg, bu = bufs(c)[6]
                for kc in range(KC):
                    S.op("pe", lambda e, kc=kc: e.matmul(ps[bg][:, 0:T], lhsT=wg_t[:, kc, m * P:(m + 1) * P], rhs=vT[:, kc, :], start=(kc == 0), stop=(kc == KC - 1)),
                         [wgk] + UT2, [f"ps{bg}"])
                for kc in range(KC):
                    S.op("pe", lambda e, kc=kc: e.matmul(ps[bu][:, 0:T], lhsT=wu_t[:, kc, m * P:(m + 1) * P], rhs=vT[:, kc, :], start=(kc == 0), stop=(kc == KC - 1)),
                         [wuk] + UT2, [f"ps{bu}"])

            def stage_a(c):
                G, Gk, ca, cak, sl, slk, (bg, bu) = bufs(c)
                S.op("pool", lambda e: e.tensor_copy(out=G[:, 0:2], in_=carry[:, c, :]), [f"carry{c}"], [Gk + "h"])
                S.op("act", lambda e: e.activation(out=G[:, 2:T + 2], in_=ps[bg][:], func=AF.Copy), [f"ps{bg}"], [Gk])
                S.op("pool", lambda e: e.tensor_copy(out=carry[:, c, :], in_=G[:, T:T + 2]), [Gk], [f"carry{c}"])
                S.op("pool", lambda e: e.tensor_scalar(out=ca[:], in0=G[:, 2:T + 2], scalar1=convw[:, c, 2:3], scalar2=convb[:, c:c + 1], op0=ALU.mult, op1=ALU.add),
                     [Gk, "cst"], [cak])

            def stage_b(c):
                G, Gk, ca, cak, sl, slk, (bg, bu) = bufs(c)
                S.op("dve", lambda e: e.scalar_tensor_tensor(out=ca[:], in0=G[:, 1:T + 1], scalar=convw[:, c, 1:2], in1=ca[:], op0=ALU.mult, op1=ALU.add),
                     [Gk, Gk + "h", "cst", cak], [cak])
                S.op("dve", lambda e: e.scalar_tensor_tensor(out=ca[:], in0=G[:, 0:T], scalar=convw[:, c, 0:1], in1=ca[:], op0=ALU.mult, op1=ALU.add),
                     [Gk, Gk + "h", "cst", cak], [cak])

            def stage_c(c):
                G, Gk, ca, cak, sl, slk, (bg, bu) = bufs(c)
                S.op("act", lambda e: e.activation(out=sl[:], in_=ca[:], func=AF.Silu), [cak], [slk])

            def stage_d(c):
                G, Gk, ca, cak, sl, slk, (bg, bu) = bufs(c)
                S.op("dve", lambda e: e.tensor_tensor(out=actT[:, c, :], in0=ps[bu][:], in1=sl[:], op=ALU.mult), [f"ps{bu}", slk], [f"actT{c}"])

            for i in range(NFC + 2):
                if i < NFC:
                    stage_pe(i)
                    stage_a(i)
                if 0 <= i - 1 < NFC:
                    stage_b(i - 1)
                    stage_c(i - 1)
                if 0 <= i - 2 < NFC:
                    stage_d(i - 2)
                yield

        def final(st):
            xb = st % 2
            for s in range(NSUB):
                ssq, rstd, sk = new_stat()
                ub = ubf[s % 2]
                ubk = f"ubf{s % 2}"
                xin = xt[xb][:, s, :]
                S.op("act", lambda e, ub=ub, xin=xin, ssq=ssq: e.activation(out=ub[:], in_=xin, func=AF.Square, accum_out=ssq), [f"xt{xb}s{s}"], [ubk, sk])
                S.op("pool", lambda e, ssq=ssq, rstd=rstd: e.tensor_scalar(out=rstd, in0=ssq, scalar1=1.0 / D, scalar2=EPS, op0=ALU.mult, op1=ALU.add), [sk], [sk])
                S.op("pool", lambda e, rstd=rstd: e.tensor_tensor(out=rstd, in0=rstd, in1=nhalf, op=ALU.pow), [sk, "nhalf"], [sk])
                S.op("dve", lambda e, xin=xin, rstd=rstd: e.scalar_tensor_tensor(out=xin, in0=xin, scalar=rstd, in1=nw3[:, 2, :], op0=ALU.mult, op1=ALU.mult),
                     [f"xt{xb}s{s}", sk, "nw3"], [f"xt{xb}s{s}"])
                r0 = st * T + s * P
                dst = out_d[r0:r0 + P, :]
                out_recs.append(S.op("sp", lambda e, dst=dst, xin=xin: e.dma_start(out=dst, in_=xin), [f"xt{xb}s{s}"], [], dma=f"o{xb}{s}"))

        norm_T(0, 0, uT, UT1)
        x_next = 1
        if nst > 1:
            x_load(1)
            x_next = 2
        for st in range(nst):
            xb = st % 2
            inproj(st)
            gens = [mixer_gen(st)]
            if st > 0:
                gens.append(down_gen(st - 1))
            interleave(gens)
            if st > 0:
                final(st - 1)
                if x_next < nst and x_next == st + 1:
                    x_load(x_next)
                    x_next += 1
            outproj_norm2(st)
            if st == 0:
                dump("hmid", xt[0][:], [f"xt0s{s}" for s in range(NSUB)])
            gens = [ffn_gateup(st)]
            if st + 1 < nst:
                gens.append(delayed(norm_gen((st + 1) % 2, 0, uT, UT1), 6))
            interleave(gens)
        interleave([down_gen(nst - 1)])
        final(nst - 1)
        dbg_recs = [r for r in S.ops["sp"] if r["dma"] and r["dma"].startswith("dbg_")]
        S.wait_all("sp", out_recs + dbg_recs)
        S.emit(nc)
    return nc


def prep_inputs(x, norm_mix_w, w_in, b_attn, lb_logits, hg_norm_w, sinks, w_out,
                norm_ffn_w, w_gate, w_up, conv_w, conv_b, w_down, final_norm_w):
    f = np.float32
    w_in0 = np.asarray(w_in, f)[0]
    aq_perm = []
    for j in range(4):
        aq_perm += list(range(2048 + j * 64, 2048 + (j + 1) * 64))
        aq_perm += list(range(2048 + (4 + j) * 64, 2048 + (5 + j) * 64))
    perm = (list(range(0, 512)) + list(range(512, 1024)) + list(range(1536, 2048)) + aq_perm
            + list(range(1024, 1536)) + list(range(2560, 2688)) + list(range(2688, 2816)))
    w_in_p = w_in0[:, np.asarray(perm)]
    w_in_r = w_in_p.reshape(KC, P, INC).transpose(1, 0, 2)
    w_out_r = np.asarray(w_out, f)[0].reshape(KC, P, D).transpose(1, 0, 2)
    w_gate_r = np.asarray(w_gate, f)[0].reshape(KC, P, DFF).transpose(1, 0, 2)
    w_up_r = np.asarray(w_up, f)[0].reshape(KC, P, DFF).transpose(1, 0, 2)
    w_down_r = np.asarray(w_down, f)[0].reshape(NFC, P, D).transpose(1, 0, 2)
    shared = {
        "w_in": np.concatenate([w_in_r[:, :, c0 + hh * 256:c0 + (hh + 1) * 256].reshape(P, -1) for c0 in W_IN_FM for hh in range(2)]
                               + [w_in_r[:, kh * 4:(kh + 1) * 4, W_IN_HI:W_IN_HI + 512].reshape(P, -1) for kh in range(2)]
                               + [w_in_r[:, :, W_IN_AKAV:W_IN_AKAV + 256].reshape(P, -1)], 1),
        "w_out": np.concatenate([w_out_r[:, kh * 4:(kh + 1) * 4, nh * 512:(nh + 1) * 512].reshape(P, -1) for nh in range(2) for kh in range(2)], 1),
        "w_gate": np.concatenate([w_gate_r[:, :, g * 256:(g + 1) * 256].reshape(P, -1) for g in range(11)], 1),
        "w_up": np.concatenate([w_up_r[:, :, g * 256:(g + 1) * 256].reshape(P, -1) for g in range(11)], 1),
        "w_down": np.concatenate([w_down_r[:, fg * 4:fg * 4 + a, nh * 512:(nh + 1) * 512].reshape(P, -1) for (nh, fg, a) in W_DOWN_SPECS], 1),
    }
    shared = {k: np.ascontiguousarray(v) for k, v in shared.items()}
    nw3 = np.stack([np.asarray(norm_mix_w, f)[0], np.asarray(norm_ffn_w, f)[0], np.asarray(final_norm_w, f)], 0)
    shared["nw3"] = np.ascontiguousarray(np.broadcast_to(nw3[None], (P, 3, D)))
    cst = np.zeros((P, NCST), f)
    lbl = np.asarray(lb_logits, f).reshape(2, 4, P)
    cst[:, C_LBL:C_LBL + 8] = lbl.transpose(2, 0, 1).reshape(P, 8)
    cst[:, C_HGW] = np.asarray(hg_norm_w, f)[0]
    cst[:, C_SINK:C_SINK + 8] = np.asarray(sinks, f)[0][None, :]
    ba = np.asarray(b_attn, f)[0]
    bqh = ba[0:512].reshape(8, 64)
    for j in range(4):
        cst[0:64, C_BQ + j] = bqh[j]
        cst[64:128, C_BQ + j] = bqh[4 + j]
    cst[:, C_BK] = ba[512:640]
    cst[:, C_CW:C_CW + 66] = np.asarray(conv_w, f)[0].reshape(3, NFC, P).transpose(2, 1, 0).reshape(P, 66)
    cst[:, C_CB:C_CB + NFC] = np.asarray(conv_b, f)[0].reshape(NFC, P).T
    cst[:, C_BV:C_BV + 128] = ba[640:768][None, :]
    shared["cst"] = cst
    return shared


def kernel(x, norm_mix_w, w_in, b_attn, lb_logits, hg_norm_w, sinks, w_out,
           norm_ffn_w, w_gate, w_up, conv_w, conv_b, w_down, final_norm_w):
    x = np.asarray(x, np.float32)
    B = x.shape[0]
    shared = prep_inputs(x, norm_mix_w, w_in, b_attn, lb_logits, hg_norm_w, sinks, w_out,
                         norm_ffn_w, w_gate, w_up, conv_w, conv_b, w_down, final_norm_w)
    nc = build_nc()
    in_maps = []
    for b in range(B):
        m = dict(shared)
        m["x"] = np.ascontiguousarray(x[b])
        in_maps.append(m)
    res = run_bass_kernel_spmd(nc, in_maps, core_ids=list(range(B)))
    return np.stack([np.asarray(r["out"], np.float32) for r in res.results], 0)
```

```python
import os
from contextlib import ExitStack

import numpy as np
import concourse.bass as bass
import concourse.mybir as mybir
from concourse.bass_utils import run_bass_kernel_spmd

F32 = mybir.dt.float32
BF16 = mybir.dt.bfloat16
ALU = mybir.AluOpType
AF = mybir.ActivationFunctionType

P = 128
D = 1024
KC = 8
T = 512
NSUB = 4
SEQ = 4096
DFF = 2816
NFC = 22
INC = 2816
EPS = 1e-6
NW = 6
WSLOT = 2048

ENGS = ["pe", "act", "dve", "pool", "sp"]
SAME_ENG_WINDOW = 4


class Sched:
    def __init__(self):
        self.ops = {e: [] for e in ENGS}
        self.last_w = {}
        self.readers = {}
        self.dma_cnt = {}
        self.seen = {c: {} for c in ENGS}
        self.seen_dma = {c: {} for c in ENGS}
        self.final_dma = []

    def op(self, eng, fn, reads=(), writes=(), dma=None):
        idx = len(self.ops[eng])
        deps = []
        for b in reads:
            w = self.last_w.get(b)
            if w is not None:
                deps.append(w)
        for b in writes:
            w = self.last_w.get(b)
            if w is not None:
                deps.append(w)
            deps.extend(self.readers.get(b, ()))
        rec = {"eng": eng, "idx": idx, "fn": fn, "waits": [], "inc": False, "dma": dma, "dval": None}
        if dma is not None:
            self.dma_cnt[dma] = self.dma_cnt.get(dma, 0) + 16
            rec["dval"] = self.dma_cnt[dma]
        comp_w = {}
        dma_w = {}
        for d in deps:
            if d["dma"] is not None:
                t = d["dma"]
                if d["dval"] > self.seen_dma[eng].get(t, 0):
                    dma_w[t] = max(dma_w.get(t, 0), d["dval"])
            else:
                p = d["eng"]
                if p == eng:
                    if eng == "pe":
                        continue
                    if d["idx"] < idx - SAME_ENG_WINDOW:
                        continue
                if d["idx"] > self.seen[eng].get(p, -1):
                    if p not in comp_w or d["idx"] > comp_w[p]["idx"]:
                        comp_w[p] = d
        for t, v in dma_w.items():
            self.seen_dma[eng][t] = v
            rec["waits"].append(("dma", t, v))
        for p, d in comp_w.items():
            self.seen[eng][p] = d["idx"]
            d["inc"] = True
            rec["waits"].append(("eng", p, d))
        self.ops[eng].append(rec)
        for b in reads:
            self.readers.setdefault(b, []).append(rec)
        for b in writes:
            self.last_w[b] = rec
            self.readers[b] = []
        return rec

    def wait_all(self, eng, recs):
        self.final_dma.append((eng, recs))

    def emit(self, nc):
        with ExitStack() as es:
            esem = {e: es.enter_context(nc.semaphore("s_" + e)) for e in ENGS}
            dsem = {t: es.enter_context(nc.semaphore("d_" + t)) for t in self.dma_cnt}
            for e in ENGS:
                c = 0
                for r in self.ops[e]:
                    if r["dma"] is None and r["inc"]:
                        c += 1
                        r["cnt"] = c
            block = es.enter_context(nc.Block())

            def run(e, eng):
                for r in self.ops[e]:
                    for w in r["waits"]:
                        if w[0] == "dma":
                            eng.wait_ge(dsem[w[1]], w[2])
                        else:
                            eng.wait_ge(esem[w[1]], w[2]["cnt"])
                    ins = r["fn"](eng)
                    if r["dma"] is not None:
                        ins.then_inc(dsem[r["dma"]], 16)
                    elif r["inc"]:
                        ins.then_inc(esem[e], 1)
                for (fe, recs) in self.final_dma:
                    if fe == e:
                        for d in recs:
                            eng.wait_ge(dsem[d["dma"]], d["dval"])

            @block.tensor
            def _(eng):
                run("pe", eng)

            @block.scalar
            def _(eng):
                run("act", eng)

            @block.vector
            def _(eng):
                run("dve", eng)

            @block.gpsimd
            def _(eng):
                run("pool", eng)

            @block.sync
            def _(eng):
                run("sp", eng)


C_LBL = 0
C_HGW = 8
C_SINK = 9
C_BQ = 17
C_BK = 21
C_CW = 22
C_CB = 88
C_BV = 110
NCST = 238


def _tiles(specs):
    out, o = [], 0
    for (a, b) in specs:
        out.append((o, a, b))
        o += a * b
    return out


W_IN_FM = [512, 1024, 1536, 0]
W_IN_HI = 2048
W_IN_AKAV = 2560
W_IN_TILES = _tiles([(KC, 256)] * 8 + [(4, 512)] * 2 + [(KC, 256)])
W_OUT_TILES = _tiles([(4, 512)] * 4)
W_GU_TILES = _tiles([(KC, 256)] * 11)
W_DOWN_SPECS = [(nh, fg, 4 if fg < 5 else 2) for nh in range(2) for fg in range(6)]
W_DOWN_TILES = _tiles([(a, 512) for (_, _, a) in W_DOWN_SPECS])
DOWN_TILES = [(4, 0), (4, 4), (4, 8), (4, 12), (4, 16), (2, 20)]


def build_nc(nst=SEQ // T, dbg=None):
    nc = bass.Bass("TRN2", target_bir_lowering=False)
    ntok = nst * T
    x_d = nc.dram_tensor("x", [ntok, D], F32, kind="ExternalInput").ap()
    win_d = nc.dram_tensor("w_in", [P, KC * INC], F32, kind="ExternalInput").ap()
    wout_d = nc.dram_tensor("w_out", [P, KC * D], F32, kind="ExternalInput").ap()
    wg_d = nc.dram_tensor("w_gate", [P, KC * DFF], F32, kind="ExternalInput").ap()
    wu_d = nc.dram_tensor("w_up", [P, KC * DFF], F32, kind="ExternalInput").ap()
    wd_d = nc.dram_tensor("w_down", [P, NFC * D], F32, kind="ExternalInput").ap()
    nw_d = nc.dram_tensor("nw3", [P, 3, D], F32, kind="ExternalInput").ap()
    cst_d = nc.dram_tensor("cst", [P, NCST], F32, kind="ExternalInput").ap()
    out_d = nc.dram_tensor("out", [ntok, D], F32, kind="ExternalOutput").ap()
    dbg_d = {}
    if dbg:
        for name, (shape, dt) in dbg.items():
            dbg_d[name] = nc.dram_tensor("dbg_" + name, shape, dt, kind="ExternalOutput").ap()

    TPS = len(W_IN_TILES) + len(W_OUT_TILES) + 2 * len(W_GU_TILES) + len(W_DOWN_TILES)
    scr_cols = KC * INC + KC * D + 2 * KC * DFF + NFC * D
    scr_d = nc.dram_tensor("wscr", [P, scr_cols], BF16, kind="Internal").ap()
    S = Sched()
    with ExitStack() as es:
        def sb(name, shape, dt):
            return es.enter_context(nc.sbuf_tensor(name, shape, dt))

        xt = [sb(f"xt{i}", [P, NSUB, D], F32) for i in range(2)]
        ubf = [sb(f"ubf{i}", [P, D], BF16) for i in range(2)]
        uT = sb("uT", [P, KC, T], BF16)
        vT = sb("vT", [P, KC, T], BF16)
        wsl = [sb(f"wsl{i}", [P, WSLOT], BF16) for i in range(NW)]
        nw3 = sb("nw3s", [P, 3, D], F32)
        cst = sb("csts", [P, NCST], F32)
        stat = sb("stat", [P, 64], F32)
        cns = sb("cns", [P, 40], F32)
        ident = sb("ident", [P, P], BF16)
        ones = sb("ones", [P, P], BF16)
        scanmask = sb("scanmask", [P, T], F32)
        hmask = sb("hmask", [P, 4, P], F32)
        amask = sb("amask", [P, 2, 4, P], BF16)
        fT = [[sb(f"f{n}{i}", [P, T], F32) for n in "ABCD"] for i in range(2)]
        eb = sb("eb", [P, 4, T], F32)
        qtT = sb("qtT", [P, 4, T], BF16)
        ktT = sb("ktT", [P, 4, T], BF16)
        ktok = sb("ktok", [P, NSUB, 512], BF16)
        vtok = sb("vtok", [P, NSUB, 512], BF16)
        gsil = sb("gsil", [P, 4, T], F32)
        Sst = sb("Sst", [P, 4, P], F32)
        Sbf = [sb(f"Sbf{i}", [P, 4, P], BF16) for i in range(3)]
        stmp = sb("stmp", [P, 4, P], F32)
        amaskf = stmp
        sTm = [sb(f"sTm{i}", [P, 4, P], BF16) for i in range(2)]
        sq = sb("sq", [P, 512], BF16)
        rs = sb("rs", [P, 512], F32)
        t1 = sb("t1", [P, 512], F32)
        aqT = sb("aqT", [P, 4, T], BF16)
        akT = sb("akT", [P, 5 * P], BF16)
        vaug = sb("vaug", [P, 5, 2, 65], BF16)
        PT = [sb(f"PT{i}", [P, 512], BF16) for i in range(4)]
        den = sb("den", [P, 8], F32)
        oatt = sb("oatt", [P, 512], BF16)
        mixT = sb("mixT", [P, KC, T], BF16)
        Gb = [sb(f"Gb{i}", [P, T + 2], F32) for i in range(2)]
        cA = [sb(f"cA{i}", [P, T], F32) for i in range(2)]
        slb = [rs, t1]
        gA = cA
        carry = sb("carry", [P, NFC, 2], F32)
        actT = sb("actT", [P, NFC, T], BF16)
        ps = [es.enter_context(nc.psum_tensor(f"ps{i}", [P, 512], F32)) for i in range(8)]

        def psb(i):
            return ps[i][:].bitcast(BF16)

        lb = cns[:, 0:4]
        oml = cns[:, 4:8]
        noml = cns[:, 8:12]
        esink = cns[:, 12:20]
        bq8 = cns[:, 20:24]
        lnoml = cns[:, 28:32]
        nhalf = cns[:, 32:33]
        hgw = cst[:, C_HGW:C_HGW + 1]
        bk = cst[:, C_BK:C_BK + 1]
        convw = cst[:, C_CW:C_CW + 66].rearrange("p (c j) -> p c j", j=3)
        convb = cst[:, C_CB:C_CB + NFC]
        bv = cst[:, C_BV:C_BV + 128]

        wseq = []

        def _add(mat, d, tiles, reps=1):
            for r in range(reps):
                pass
            for ti, (o, a, b) in enumerate(tiles):
                wseq.append((d[:, o:o + a * b], a, b, (mat, ti)))

        def _add_down():
            nt = len(DOWN_TILES)
            for nh in range(2):
                for pair in range(2):
                    for ti in range(nt):
                        o, a, b = W_DOWN_TILES[nh * nt + ti]
                        wseq.append((wd_d[:, o:o + a * b], a, b, ("d", nh * nt + ti)))

        for st in range(nst):
            _add("i", win_d, W_IN_TILES)
            if st > 0:
                _add_down()
            _add("o", wout_d, W_OUT_TILES)
            for g, (o, a, b) in enumerate(W_GU_TILES):
                wseq.append((wg_d[:, o:o + a * b], a, b, ("g", g)))
                wseq.append((wu_d[:, o:o + a * b], a, b, ("u", g)))
        _add_down()
        wstate = {"loaded": 0, "used": 0, "released": 0}
        scr_off = {}
        scr_next = [0]
        last_use = {}
        for j, (_, a, b, sid) in enumerate(wseq):
            last_use[sid] = j

        def w_pump():
            while wstate["loaded"] < min(len(wseq), wstate["released"] + NW):
                j = wstate["loaded"]
                src, a, b, sid = wseq[j]
                slot = j % NW
                dst = wsl[slot][:, 0:a * b]
                if sid not in scr_off:
                    scr_off[sid] = scr_next[0]
                    scr_next[0] += a * b
                    scr = scr_d[:, scr_off[sid]:scr_off[sid] + a * b]
                    S.op("pool", (lambda e, dst=dst, src=src: e.dma_start(out=dst, in_=src)), [], [f"w{slot}"], dma=f"w{slot}")
                    if last_use[sid] > j:
                        S.op("sp", (lambda e, dst=dst, scr=scr: e.dma_start(out=scr, in_=dst)), [f"w{slot}"], [f"scr{sid}"], dma=f"wb{slot}")
                else:
                    scr = scr_d[:, scr_off[sid]:scr_off[sid] + a * b]
                    S.op("pool", (lambda e, dst=dst, scr=scr: e.dma_start(out=dst, in_=scr)), [f"scr{sid}"], [f"w{slot}"], dma=f"w{slot}")
                wstate["loaded"] += 1

        def w_next(hold=False):
            if not hold:
                w_rel(wstate["used"] - wstate["released"])
            i = wstate["used"]
            assert i < wstate["loaded"], "weight tile not yet loaded: too many tiles held"
            wstate["used"] += 1
            slot = i % NW
            _, a, b, _sid = wseq[i]
            return wsl[slot][:, 0:a * b].rearrange("p (a b) -> p a b", b=b), f"w{slot}"

        def w_rel(n=1):
            wstate["released"] += n
            w_pump()

        S.op("sp", lambda e: e.dma_start(out=cst[:], in_=cst_d), [], ["cst"], dma="c0")
        S.op("sp", lambda e: e.dma_start(out=nw3[:], in_=nw_d), [], ["nw3"], dma="c1")
        xrecs = {}

        def x_load(st):
            b = st % 2
            src = x_d[st * T:(st + 1) * T, :].rearrange("(s p) d -> p s d", p=P)
            xrecs[st] = S.op("sp", lambda e: e.dma_start(out=xt[b][:], in_=src), [],
                             [f"xt{b}s{s}" for s in range(NSUB)], dma=f"x{b}")

        x_load(0)
        w_pump()

        tmp4 = cns[:, 24:28]
        S.op("dve", lambda e: e.tensor_tensor(out=tmp4, in0=cst[:, C_LBL + 4:C_LBL + 8], in1=cst[:, C_LBL:C_LBL + 4], op=ALU.subtract), ["cst"], ["tmp4"])
        S.op("act", lambda e: e.activation(out=tmp4, in_=tmp4, func=AF.Exp), ["tmp4"], ["tmp4"])
        S.op("dve", lambda e: e.tensor_scalar_add(out=lb, in0=tmp4, scalar1=1.0), ["tmp4"], ["lb"])
        S.op("dve", lambda e: e.reciprocal(out=lb, in_=lb), ["lb"], ["lb"])
        S.op("dve", lambda e: e.tensor_tensor(out=oml, in0=tmp4, in1=lb, op=ALU.mult), ["tmp4", "lb"], ["oml"])
        S.op("dve", lambda e: e.tensor_scalar_mul(out=noml, in0=oml, scalar1=-1.0), ["oml"], ["noml"])
        S.op("act", lambda e: e.activation(out=lnoml, in_=oml, func=AF.Ln), ["oml"], ["lnoml"])
        S.op("act", lambda e: e.activation(out=esink, in_=cst[:, C_SINK:C_SINK + 8], func=AF.Exp), ["cst"], ["esink"])
        S.op("dve", lambda e: e.tensor_scalar_mul(out=bq8, in0=cst[:, C_BQ:C_BQ + 4], scalar1=0.125), ["cst"], ["bq8"])
        S.op("pool", lambda e: e.memset(nhalf, -0.5), [], ["nhalf"])
        S.op("pool", lambda e: e.memset(ident[:], 1.0), [], ["ident"])
        S.op("pool", lambda e: e.affine_select(out=ident[:], in_=ident[:], pattern=[[-1, P]], compare_op=ALU.is_equal, fill=0.0, base=0, channel_multiplier=1), ["ident"], ["ident"])
        S.op("pool", lambda e: e.memset(ones[:], 1.0), [], ["ones"])
        S.op("pool", lambda e: e.memset(scanmask[:], 1.0), [], ["scanmask"])
        smv = scanmask[:].rearrange("p (c t) -> p c t", t=64)
        S.op("pool", lambda e: e.memset(smv[:, :, 0:1], 0.0), ["scanmask"], ["scanmask"])
        S.op("pool", lambda e: e.memset(hmask[:], 1.0), [], ["hmask"])
        S.op("pool", lambda e: e.affine_select(out=hmask[:], in_=hmask[:], pattern=[[0, 4], [1, P]], compare_op=ALU.is_ge, fill=0.0, base=0, channel_multiplier=-1), ["hmask"], ["hmask"])
        S.op("pool", lambda e: e.memset(hmask[0:64, :, 64:128], 0.0), ["hmask"], ["hmask"])
        S.op("pool", lambda e: e.memset(amaskf[:], 1.0), [], ["stmp"])
        S.op("pool", lambda e: e.affine_select(out=amaskf[:], in_=amaskf[:], pattern=[[0, 4], [-1, P]], compare_op=ALU.is_ge, fill=0.0, base=-1, channel_multiplier=1), ["stmp"], ["stmp"])
        S.op("dve", lambda e: e.tensor_copy(out=amask[:, 0], in_=amaskf[:]), ["stmp"], ["amask"])
        S.op("pool", lambda e: e.memset(amaskf[:], 1.0), ["stmp"], ["stmp"])
        S.op("pool", lambda e: e.affine_select(out=amaskf[:], in_=amaskf[:], pattern=[[0, 4], [1, P]], compare_op=ALU.is_ge, fill=0.0, base=0, channel_multiplier=-1), ["stmp"], ["stmp"])
        S.op("dve", lambda e: e.tensor_copy(out=amask[:, 1], in_=amaskf[:]), ["stmp"], ["amask"])
        S.op("pool", lambda e: e.memset(Sst[:], 0.0), [], ["Sst"])
        S.op("pool", lambda e: e.memset(Sbf[0][:], 0.0), [], ["Sbf0"])
        S.op("pool", lambda e: e.memset(carry[:], 0.0), [], ["carry"])
        S.op("pool", lambda e: e.memset(vaug[:], 1.0), [], ["vaug"])
        S.op("pool", lambda e: e.memset(akT[:, 0:P], 0.0), [], ["akT0"])

        stat_ctr = [0]

        def new_stat():
            i = stat_ctr[0] % 16
            stat_ctr[0] += 1
            return stat[:, 4 * i:4 * i + 1], stat[:, 4 * i + 1:4 * i + 2], f"stat{i}"

        tr_bank = [0]
        dense_bank = [0]

        def next_dense_bank():
            b = [0, 1, 4, 6, 2, 5, 7, 3][dense_bank[0] % 8]
            dense_bank[0] += 1
            return b

        def dump(name, ap, keys):
            if dbg and name in dbg_d:
                S.op("sp", lambda e: e.dma_start(out=dbg_d[name], in_=ap), keys, [], dma="dbg_" + name)

        out_recs = []
        SC_Q = 128 ** -0.5
        UT1 = [f"uT{s}" for s in range(NSUB)]
        UT2 = [f"vT{s}" for s in range(NSUB)]

        def norm_a(xb, nwi, s):
            ssq, rstd, sk = new_stat()
            ub = ubf[s % 2]
            ubk = f"ubf{s % 2}"
            xin = xt[xb][:, s, :]
            S.op("act", lambda e: e.activation(out=ub[:], in_=xin, func=AF.Square, accum_out=ssq), [f"xt{xb}s{s}"], [ubk, sk])
            S.op("pool", lambda e: e.tensor_scalar(out=rstd, in0=ssq, scalar1=1.0 / D, scalar2=EPS, op0=ALU.mult, op1=ALU.add), [sk], [sk])
            S.op("pool", lambda e: e.tensor_tensor(out=rstd, in0=rstd, in1=nhalf, op=ALU.pow), [sk, "nhalf"], [sk])
            S.op("dve", lambda e: e.scalar_tensor_tensor(out=ub[:], in0=xin, scalar=rstd, in1=nw3[:, nwi, :], op0=ALU.mult, op1=ALU.mult),
                 [f"xt{xb}s{s}", sk, "nw3"], [ubk])

        def norm_b(s, dstT, dkeys):
            ub = ubf[s % 2]
            ubk = f"ubf{s % 2}"
            bank = tr_bank[0] % 2
            tr_bank[0] += 1
            pv = psb(bank).rearrange("p (c t) -> p c t", t=P)
            for c in range(KC):
                S.op("pe", lambda e, c=c: e.transpose(out=pv[:, c, :], in_=ub[:, c * P:(c + 1) * P], identity=ident[:]), [ubk, "ident"], [f"ps{bank}"])
            dst = dstT[:, :, s * P:(s + 1) * P]
            S.op("act", lambda e: e.activation(out=dst, in_=pv, func=AF.Copy), [f"ps{bank}"], [dkeys[s]])

        def norm_gen(xb, nwi, dstT, dkeys):
            for s in range(NSUB):
                norm_a(xb, nwi, s)
                yield
                yield
                norm_b(s, dstT, dkeys)
                yield

        def norm_T(xb, nwi, dstT, dkeys):
            for _ in norm_gen(xb, nwi, dstT, dkeys):
                pass

        def interleave(gens):
            active = list(gens)
            while active:
                nxt_active = []
                for g in active:
                    try:
                        next(g)
                        nxt_active.append(g)
                    except StopIteration:
                        pass
                active = nxt_active

        def fm_chunk(w, wk, m, bank, srcT, skeys):
            for kc in range(KC):
                S.op("pe", lambda e, kc=kc: e.matmul(ps[bank][:, 0:T], lhsT=w[:, kc, m * P:(m + 1) * P], rhs=srcT[:, kc, :], start=(kc == 0), stop=(kc == KC - 1)),
                     [wk] + skeys, [f"ps{bank}"])

        def stages_f(h, bank):
            fA, fB, fC, fD = fT[h % 2]
            kA, kB, kC_, kD = [f"f{n}{h % 2}" for n in "ABCD"]
            pk = f"ps{bank}"

            def s1():
                S.op("act", lambda e: e.activation(out=fA[:], in_=ps[bank][:], func=AF.Exp, scale=-1.0), [pk], [kA])
                S.op("act", lambda e: e.activation(out=fB[:], in_=fA[:], func=AF.Ln, bias=1.0), [kA], [kB])
                S.op("act", lambda e: e.activation(out=fC[:], in_=fA[:], func=AF.Ln, scale=lb[:, h:h + 1], bias=1.0), [kA, "lb"], [kC_])

            def s2():
                S.op("dve", lambda e: e.tensor_tensor(out=fA[:], in0=ps[bank][:], in1=fB[:], op=ALU.add), [pk, kB], [kA])
                S.op("dve", lambda e: e.tensor_tensor(out=fC[:], in0=fC[:], in1=fB[:], op=ALU.subtract), [kC_, kB], [kC_])
                S.op("dve", lambda e: e.tensor_tensor_scan(out=fD[:], data0=scanmask[:], data1=fC[:], initial=0.0, op0=ALU.mult, op1=ALU.add),
                     [kC_, "scanmask"], [kD])
                S.op("dve", lambda e: e.tensor_tensor(out=fA[:], in0=fA[:], in1=fD[:], op=ALU.add), [kA, kD], [kA])

            def s3():
                S.op("act", lambda e: e.activation(out=ktT[:, h, :], in_=fA[:], func=AF.Exp, scale=-1.0, bias=lnoml[:, h:h + 1]), [kA, "lnoml"], [f"ktT{h}"])
                S.op("act", lambda e: e.activation(out=eb[:, h, :], in_=fD[:], func=AF.Exp), [kD], [f"eb{h}"])
            return s1, s2, s3

        def stages_g(h, bank):
            g = gA[h % 2]
            gk = f"cA{h % 2}"

            def s1():
                S.op("act", lambda e: e.activation(out=g[:], in_=ps[bank][:], func=AF.Exp, scale=-1.0), [f"ps{bank}"], [gk])
                S.op("act", lambda e: e.activation(out=g[:], in_=g[:], func=AF.Ln, bias=1.0), [gk], [gk])
                S.op("act", lambda e: e.activation(out=g[:], in_=g[:], func=AF.Exp, scale=-1.0), [gk], [gk])

            def s2():
                S.op("dve", lambda e: e.tensor_tensor(out=gsil[:, h, :], in0=ps[bank][:], in1=g[:], op=ALU.mult), [f"ps{bank}", gk], [f"gsil{h}"])
            return s1, s2, None

        def stages_aq(j, bank):
            def s1():
                S.op("act", lambda e: e.activation(out=aqT[:, j, :], in_=ps[bank][:], func=AF.Identity, scale=0.125, bias=bq8[:, j:j + 1]),
                     [f"ps{bank}", "bq8"], [f"aqT{j}"])
            return s1, None, None

        def stages_q(h, bank):
            def s2():
                S.op("dve", lambda e: e.scalar_tensor_tensor(out=qtT[:, h, :], in0=ps[bank][:], scalar=SC_Q, in1=eb[:, h, :], op0=ALU.mult, op1=ALU.mult),
                     [f"ps{bank}", f"eb{h}"], [f"qtT{h}"])
            return None, s2, None

        def inproj(st):
            dense_bank[0] = 0
            chunks = [(fn, hh, m) for fn in (stages_f, stages_g, stages_aq, stages_q) for hh in range(2) for m in range(2)]
            stg = {}
            wcur = [None]
            for i in range(len(chunks) + 2):
                if i < len(chunks):
                    fn, hh, m = chunks[i]
                    if m == 0:
                        wcur[0] = w_next()
                    w, wk = wcur[0]
                    bank = next_dense_bank()
                    fm_chunk(w, wk, m, bank, uT, UT1)
                    stg[i] = fn(hh * 2 + m, bank)
                if 0 <= i - 2 < len(chunks) and stg[i - 2][2]:
                    stg[i - 2][2]()
                if i < len(chunks) and stg[i][0]:
                    stg[i][0]()
                if 0 <= i - 1 < len(chunks) and stg[i - 1][1]:
                    stg[i - 1][1]()
            wh = [w_next(), w_next(hold=True)]
            for s in range(NSUB):
                bank = next_dense_bank()
                for kc in range(KC):
                    w, wk = wh[kc // 4]
                    S.op("pe", lambda e, kc=kc, s=s, bank=bank, w=w: e.matmul(ps[bank][:, 0:512], lhsT=uT[:, kc, s * P:(s + 1) * P], rhs=w[:, kc % 4, :], start=(kc == 0), stop=(kc == KC - 1)),
                         [wk, f"uT{s}"], [f"ps{bank}"])
                S.op("act", lambda e, s=s, bank=bank: e.activation(out=vtok[:, s, :], in_=ps[bank][:], func=AF.Copy), [f"ps{bank}"], [f"vtok{s}"])
            w, wk = w_next()
            bank = next_dense_bank()
            fm_chunk(w, wk, 0, bank, uT, UT1)
            S.op("act", lambda e, bank=bank: e.activation(out=akT[:, P:5 * P], in_=ps[bank][:], func=AF.Identity, bias=bk),
                 [f"ps{bank}", "cst"], [f"akT{i}" for i in range(1, 5)])
            bank = next_dense_bank()
            for s in range(NSUB):
                for kc in range(KC):
                    S.op("pe", lambda e, kc=kc, s=s, bank=bank, w=w: e.matmul(ps[bank][:, s * P:(s + 1) * P], lhsT=uT[:, kc, s * P:(s + 1) * P], rhs=w[:, kc, P:2 * P], start=(kc == 0), stop=(kc == KC - 1)),
                         [wk, f"uT{s}"], [f"ps{bank}"])
            for s in range(NSUB):
                S.op("dve", lambda e, s=s, bank=bank: e.tensor_tensor(out=vaug[:, 1 + s, :, 0:64], in0=ps[bank][:, s * P:(s + 1) * P].rearrange("p (a d) -> p a d", d=64),
                                                                      in1=bv.rearrange("p (a d) -> p a d", d=64), op=ALU.add),
                     [f"ps{bank}", "cst"], [f"vaug{1 + s}"])

        def hg_gen(st, s):
            cs = slice(s * P, (s + 1) * P)
            pv = psb(7).rearrange("p (c t) -> p c t", t=P)
            for h in range(4):
                S.op("pe", lambda e, h=h: e.transpose(out=pv[:, h, :], in_=ktT[:, h, cs], identity=ident[:]), [f"ktT{h}", "ident"], ["ps7"])
            for h in range(4):
                S.op("pe", lambda e, h=h: e.matmul(ps[6][:, h * P:(h + 1) * P], lhsT=ktT[:, h, cs], rhs=qtT[:, h, cs], start=True, stop=True),
                     [f"ktT{h}", f"qtT{h}"], ["ps6"])
            yield
            S.op("act", lambda e: e.activation(out=ktok[:, s, :].rearrange("p (h k) -> p h k", k=P), in_=pv[:, 0:4, :], func=AF.Copy), ["ps7"], [f"ktok{s}"])
            sm = sTm[s % 2]
            smk = f"sTm{s % 2}"
            S.op("dve", lambda e: e.tensor_tensor(out=sm[:], in0=ps[6][:].rearrange("p (h t) -> p h t", t=P), in1=hmask[:], op=ALU.mult), ["ps6", "hmask"], [smk])
            yield
            n0 = (st * NSUB + s) * 2
            for ci in range(2):
                pr = slice(ci * 64, (ci + 1) * 64)
                for h in range(4):
                    S.op("pe", lambda e, h=h, pr=pr: e.matmul(ps[2][:, h * P:(h + 1) * P], lhsT=ktok[pr, s, h * P:(h + 1) * P], rhs=vtok[pr, s, h * P:(h + 1) * P], start=True, stop=True),
                         [f"ktok{s}", f"vtok{s}"], ["ps2"])
                yield
                tl = s * P + ci * 64 + 63
                nxt = (n0 + ci + 1) % 3
                S.op("dve", lambda e: e.tensor_tensor(out=stmp[:], in0=ps[2][:].rearrange("p (h v) -> p h v", v=P), in1=Sst[:], op=ALU.add), ["ps2", "Sst"], ["stmp"])
                S.op("dve", lambda e, tl=tl: e.tensor_tensor(out=Sst[:], in0=stmp[:], in1=eb[:, :, tl:tl + 1].to_broadcast([P, 4, P]), op=ALU.mult),
                     ["stmp"] + [f"eb{h}" for h in range(4)], ["Sst"])
                S.op("act", lambda e, nxt=nxt: e.activation(out=Sbf[nxt][:], in_=Sst[:], func=AF.Copy), ["Sst"], [f"Sbf{nxt}"])
                yield
            for h in range(4):
                S.op("pe", lambda e, h=h: e.matmul(ps[7][:, h * P:(h + 1) * P], lhsT=vtok[:, s, h * P:(h + 1) * P], rhs=sm[:, h, :], start=True, stop=False),
                     [f"vtok{s}", smk], ["ps7"])
                for ci in range(2):
                    tcol = slice(s * P + ci * 64, s * P + ci * 64 + 64)
                    cur = (n0 + ci) % 3
                    S.op("pe", lambda e, h=h, ci=ci, tcol=tcol, cur=cur: e.matmul(ps[7][:, h * P + ci * 64:h * P + ci * 64 + 64], lhsT=Sbf[cur][:, h, :], rhs=qtT[:, h, tcol], start=False, stop=(ci == 1)),
                         [f"Sbf{cur}", f"qtT{h}"], ["ps7"])
            yield
            S.op("act", lambda e: e.activation(out=sq[:], in_=ps[7][:], func=AF.Square), ["ps7"], ["sq"])
            yield
            S.op("pe", lambda e: e.matmul(ps[6][:, 0:512], lhsT=ones[:], rhs=sq[:], start=True, stop=True), ["ones", "sq"], ["ps6"])
            yield
            S.op("act", lambda e: e.activation(out=rs[:], in_=ps[6][:], func=AF.Ln, scale=1.0 / P, bias=EPS), ["ps6"], ["rs"])
            S.op("act", lambda e: e.activation(out=rs[:], in_=rs[:], func=AF.Exp, scale=-0.5), ["rs"], ["rs"])
            yield
            S.op("dve", lambda e: e.tensor_tensor(out=t1[:], in0=ps[7][:], in1=rs[:], op=ALU.mult), ["ps7", "rs"], ["t1"])
            S.op("dve", lambda e: e.scalar_tensor_tensor(out=mixT[:, 0:4, cs], in0=t1[:].rearrange("p (h t) -> p h t", t=P), scalar=hgw, in1=gsil[:, :, cs], op0=ALU.mult, op1=ALU.mult),
                 ["t1", "cst"] + [f"gsil{h}" for h in range(4)], [f"mixT{s}a"])

        def at_gen(st, s):
            nb = st * NSUB + s
            cs = slice(s * P, (s + 1) * P)
            blks = [(1, 1 + s)] if nb == 0 else [(0, s), (1, 1 + s)]
            for kvh in range(2):
                pr = slice(kvh * 64, (kvh + 1) * 64)
                for (mi, slot) in blks:
                    bank = 4 + mi
                    S.op("pe", lambda e, slot=slot, bank=bank, pr=pr: e.matmul(ps[bank][:].rearrange("p (j q) -> p j q", q=P), lhsT=akT[pr, slot * P:(slot + 1) * P], rhs=aqT[pr, :, cs], start=True, stop=True),
                         [f"akT{slot}"] + [f"aqT{j}" for j in range(4)], [f"ps{bank}"])
                yield
                for (mi, slot) in blks:
                    bank = 4 + mi
                    pt = PT[kvh * 2 + mi]
                    ptk = f"PT{kvh * 2 + mi}"
                    S.op("act", lambda e, pt=pt, bank=bank: e.activation(out=pt[:], in_=ps[bank][:], func=AF.Exp), [f"ps{bank}"], [ptk])
                    S.op("dve", lambda e, pt=pt, mi=mi: e.tensor_tensor(out=pt[:], in0=pt[:], in1=amask[:, mi].rearrange("p h q -> p (h q)"), op=ALU.mult), [ptk, "amask"], [ptk])
                yield
                for j in range(4):
                    col = j * 65
                    for bi, (mi, slot) in enumerate(blks):
                        pt = PT[kvh * 2 + mi]
                        S.op("pe", lambda e, pt=pt, j=j, slot=slot, col=col, bi=bi, kvh=kvh: e.matmul(ps[1][:, col:col + 65], lhsT=pt[:, j * P:(j + 1) * P], rhs=vaug[:, slot, kvh, :], start=(bi == 0), stop=(bi == len(blks) - 1)),
                             [f"PT{kvh * 2 + mi}", f"vaug{slot}"], ["ps1"])
                yield
                pvv = ps[1][:, 0:260].rearrange("p (h d) -> p h d", d=65)
                dn = den[:, kvh * 4:kvh * 4 + 4]
                S.op("dve", lambda e, dn=dn, kvh=kvh: e.tensor_tensor(out=dn, in0=pvv[:, :, 64], in1=esink[:, kvh * 4:kvh * 4 + 4], op=ALU.add), ["ps1", "esink"], [f"den{kvh}"])
                S.op("dve", lambda e, dn=dn: e.reciprocal(out=dn, in_=dn), [f"den{kvh}"], [f"den{kvh}"])
                S.op("dve", lambda e, dn=dn, kvh=kvh: e.tensor_tensor(out=oatt[:, kvh * 256:(kvh + 1) * 256].rearrange("p (h d) -> p h d", d=64), in0=pvv[:, :, 0:64],
                                                             in1=dn.unsqueeze(2).to_broadcast([P, 4, 64]), op=ALU.mult),
                     ["ps1", f"den{kvh}"], [f"oatt{kvh}"])
                yield
            pv1 = psb(1).rearrange("p (c t) -> p c t", t=P)
            for c in range(4):
                S.op("pe", lambda e, c=c: e.transpose(out=pv1[:, c, :], in_=oatt[:, c * P:(c + 1) * P], identity=ident[:]), [f"oatt{c // 2}", "ident"], ["ps1"])
            yield
            S.op("act", lambda e: e.activation(out=mixT[:, 4:8, cs], in_=pv1[:, 0:4, :], func=AF.Copy), ["ps1"], [f"mixT{s}b"])

        def mixer_gen(st):
            for s in range(NSUB):
                active = [hg_gen(st, s), at_gen(st, s)]
                while active:
                    nxt_active = []
                    for g in active:
                        try:
                            next(g)
                            nxt_active.append(g)
                        except StopIteration:
                            pass
                    active = nxt_active
                    yield
            S.op("act", lambda e: e.activation(out=akT[:, 0:P], in_=akT[:, 4 * P:5 * P], func=AF.Copy), ["akT4"], ["akT0"])
            S.op("pool", lambda e: e.tensor_copy(out=vaug[:, 0], in_=vaug[:, 4]), ["vaug4"], ["vaug0"])

        def down_gen(st):
            xb = st % 2
            for nh in range(2):
                for pair in range(2):
                    for ti, (nfc, c0) in enumerate(DOWN_TILES):
                        w, wk = w_next()
                        for si in range(2):
                            s = pair * 2 + si
                            bank = (0, 3)[si]
                            for fc in range(nfc):
                                c = c0 + fc
                                S.op("pe", lambda e, fc=fc, c=c, s=s, bank=bank, w=w: e.matmul(ps[bank][:, 0:512], lhsT=actT[:, c, s * P:(s + 1) * P], rhs=w[:, fc, :], start=(c == 0), stop=(c == NFC - 1)),
                                     [wk, f"actT{c}"], [f"ps{bank}"])
                            yield
                    for si in range(2):
                        s = pair * 2 + si
                        bank = (0, 3)[si]
                        xs = xt[xb][:, s, nh * 512:(nh + 1) * 512]
                        S.op("dve", lambda e, xs=xs, bank=bank: e.tensor_tensor(out=xs, in0=ps[bank][:], in1=xs, op=ALU.add), [f"ps{bank}", f"xt{xb}s{s}"], [f"xt{xb}s{s}"])

        def delayed(gen, n):
            for _ in range(n):
                yield
            yield from gen

        def tok_proj_gen(xb, srcT, skeys_fn, nk, tiles):
            for nh in range(2):
                for ti, (nfc, c0) in enumerate(tiles):
                    w, wk = w_next()
                    for s in range(NSUB):
                        bank = 2 + s
                        for fc in range(nfc):
                            c = c0 + fc
                            S.op("pe", lambda e, fc=fc, c=c, s=s, bank=bank, w=w: e.matmul(ps[bank][:, 0:512], lhsT=srcT[:, c, s * P:(s + 1) * P], rhs=w[:, fc, :], start=(c == 0), stop=(c == nk - 1)),
                                 [wk] + skeys_fn(s, c), [f"ps{bank}"])
                        yield
                for s in range(NSUB):
                    bank = 2 + s
                    xs = xt[xb][:, s, nh * 512:(nh + 1) * 512]
                    S.op("dve", lambda e, xs=xs, bank=bank: e.tensor_tensor(out=xs, in0=ps[bank][:], in1=xs, op=ALU.add), [f"ps{bank}", f"xt{xb}s{s}"], [f"xt{xb}s{s}"])

        def outproj_norm2(st):
            xb = st % 2
            wt = [w_next()] + [w_next(hold=True) for _ in range(3)]
            for s in range(NSUB):
                for nh in range(2):
                    bank = 2 + (2 * s + nh) % 4
                    for kc in range(KC):
                        w, wk = wt[nh * 2 + kc // 4]
                        S.op("pe", lambda e, kc=kc, s=s, bank=bank, w=w: e.matmul(ps[bank][:, 0:512], lhsT=mixT[:, kc, s * P:(s + 1) * P], rhs=w[:, kc % 4, :], start=(kc == 0), stop=(kc == KC - 1)),
                             [wk, f"mixT{s}a", f"mixT{s}b"], [f"ps{bank}"])
                    xs = xt[xb][:, s, nh * 512:(nh + 1) * 512]
                    S.op("dve", lambda e, xs=xs, bank=bank: e.tensor_tensor(out=xs, in0=ps[bank][:], in1=xs, op=ALU.add), [f"ps{bank}", f"xt{xb}s{s}"], [f"xt{xb}s{s}"])
                norm_a(xb, 1, s)
                if s > 0:
                    norm_b(s - 1, vT, UT2)
            norm_b(NSUB - 1, vT, UT2)

        def ffn_gateup(st):
            wts = {}

            def bufs(c):
                par = c % 2
                return Gb[par], f"Gb{par}", cA[par], f"cA{par}", slb[par], ["rs", "t1"][par], [(2, 3), (4, 5), (6, 7)][c % 3]

            def stage_pe(c):
                g, m = c // 2, c % 2
                if m == 0:
                    wts[g] = (w_next(), w_next(hold=True))
                (wg_t, wgk), (wu_t, wuk) = wts[g]
                bg, bu = bufs(c)[6]
                for kc in range(KC):
                    S.op("pe", lambda e, kc=kc: e.matmul(ps[bg][:, 0:T], lhsT=wg_t[:, kc, m * P:(m + 1) * P], rhs=vT[:, kc, :], start=(kc == 0), stop=(kc == KC - 1)),
                         [wgk] + UT2, [f"ps{bg}"])
                for kc in range(KC):
                    S.op("pe", lambda e, kc=kc: e.matmul(ps[bu][:, 0:T], lhsT=wu_t[:, kc, m * P:(m + 1) * P], rhs=vT[:, kc, :], start=(kc == 0), stop=(kc == KC - 1)),
                         [wuk] + UT2, [f"ps{bu}"])

            def stage_a(c):
                G, Gk, ca, cak, sl, slk, (bg, bu) = bufs(c)
                S.op("pool", lambda e: e.tensor_copy(out=G[:, 0:2], in_=carry[:, c, :]), [f"carry{c}"], [Gk + "h"])
                S.op("act", lambda e: e.activation(out=G[:, 2:T + 2], in_=ps[bg][:], func=AF.Copy), [f"ps{bg}"], [Gk])
                S.op("pool", lambda e: e.tensor_copy(out=carry[:, c, :], in_=G[:, T:T + 2]), [Gk], [f"carry{c}"])
                S.op("pool", lambda e: e.tensor_scalar(out=ca[:], in0=G[:, 2:T + 2], scalar1=convw[:, c, 2:3], scalar2=convb[:, c:c + 1], op0=ALU.mult, op1=ALU.add),
                     [Gk, "cst"], [cak])

            def stage_b(c):
                G, Gk, ca, cak, sl, slk, (bg, bu) = bufs(c)
                S.op("dve", lambda e: e.scalar_tensor_tensor(out=ca[:], in0=G[:, 1:T + 1], scalar=convw[:, c, 1:2], in1=ca[:], op0=ALU.mult, op1=ALU.add),
                     [Gk, Gk + "h", "cst", cak], [cak])
                S.op("dve", lambda e: e.scalar_tensor_tensor(out=ca[:], in0=G[:, 0:T], scalar=convw[:, c, 0:1], in1=ca[:], op0=ALU.mult, op1=ALU.add),
                     [Gk, Gk + "h", "cst", cak], [cak])

            def stage_c(c):
                G, Gk, ca, cak, sl, slk, (bg, bu) = bufs(c)
                S.op("act", lambda e: e.activation(out=sl[:], in_=ca[:], func=AF.Silu), [cak], [slk])

            def stage_d(c):
                G, Gk, ca, cak, sl, slk, (bg, bu) = bufs(c)
                S.op("dve", lambda e: e.tensor_tensor(out=actT[:, c, :], in0=ps[bu][:], in1=sl[:], op=ALU.mult), [f"ps{bu}", slk], [f"actT{c}"])

            for i in range(NFC + 2):
                if i < NFC:
                    stage_pe(i)
                    stage_a(i)
                if 0 <= i - 1 < NFC:
                    stage_b(i - 1)
                    stage_c(i - 1)
                if 0 <= i - 2 < NFC:
                    stage_d(i - 2)
                yield

        def final(st):
            xb = st % 2
            for s in range(NSUB):
                ssq, rstd, sk = new_stat()
                ub = ubf[s % 2]
                ubk = f"ubf{s % 2}"
                xin = xt[xb][:, s, :]
                S.op("act", lambda e, ub=ub, xin=xin, ssq=ssq: e.activation(out=ub[:], in_=xin, func=AF.Square, accum_out=ssq), [f"xt{xb}s{s}"], [ubk, sk])
                S.op("pool", lambda e, ssq=ssq, rstd=rstd: e.tensor_scalar(out=rstd, in0=ssq, scalar1=1.0 / D, scalar2=EPS, op0=ALU.mult, op1=ALU.add), [sk], [sk])
                S.op("pool", lambda e, rstd=rstd: e.tensor_tensor(out=rstd, in0=rstd, in1=nhalf, op=ALU.pow), [sk, "nhalf"], [sk])
                S.op("dve", lambda e, xin=xin, rstd=rstd: e.scalar_tensor_tensor(out=xin, in0=xin, scalar=rstd, in1=nw3[:, 2, :], op0=ALU.mult, op1=ALU.mult),
                     [f"xt{xb}s{s}", sk, "nw3"], [f"xt{xb}s{s}"])
                r0 = st * T + s * P
                dst = out_d[r0:r0 + P, :]
                out_recs.append(S.op("sp", lambda e, dst=dst, xin=xin: e.dma_start(out=dst, in_=xin), [f"xt{xb}s{s}"], [], dma=f"o{xb}{s}"))

        norm_T(0, 0, uT, UT1)
        x_next = 1
        if nst > 1:
            x_load(1)
            x_next = 2
        for st in range(nst):
            xb = st % 2
            inproj(st)
            gens = [mixer_gen(st)]
            if st > 0:
                gens.append(down_gen(st - 1))
            interleave(gens)
            if st > 0:
                final(st - 1)
                if x_next < nst and x_next == st + 1:
                    x_load(x_next)
                    x_next += 1
            outproj_norm2(st)
            if st == 0:
                dump("hmid", xt[0][:], [f"xt0s{s}" for s in range(NSUB)])
            gens = [ffn_gateup(st)]
            if st + 1 < nst:
                gens.append(delayed(norm_gen((st + 1) % 2, 0, uT, UT1), 6))
            interleave(gens)
        interleave([down_gen(nst - 1)])
        final(nst - 1)
        dbg_recs = [r for r in S.ops["sp"] if r["dma"] and r["dma"].startswith("dbg_")]
        S.wait_all("sp", out_recs + dbg_recs)
        S.emit(nc)
    return nc


def prep_inputs(x, norm_mix_w, w_in, b_attn, lb_logits, hg_norm_w, sinks, w_out,
                norm_ffn_w, w_gate, w_up, conv_w, conv_b, w_down, final_norm_w):
    f = np.float32
    w_in0 = np.asarray(w_in, f)[0]
    aq_perm = []
    for j in range(4):
        aq_perm += list(range(2048 + j * 64, 2048 + (j + 1) * 64))
        aq_perm += list(range(2048 + (4 + j) * 64, 2048 + (5 + j) * 64))
    perm = (list(range(0, 512)) + list(range(512, 1024)) + list(range(1536, 2048)) + aq_perm
            + list(range(1024, 1536)) + list(range(2560, 2688)) + list(range(2688, 2816)))
    w_in_p = w_in0[:, np.asarray(perm)]
    w_in_r = w_in_p.reshape(KC, P, INC).transpose(1, 0, 2)
    w_out_r = np.asarray(w_out, f)[0].reshape(KC, P, D).transpose(1, 0, 2)
    w_gate_r = np.asarray(w_gate, f)[0].reshape(KC, P, DFF).transpose(1, 0, 2)
    w_up_r = np.asarray(w_up, f)[0].reshape(KC, P, DFF).transpose(1, 0, 2)
    w_down_r = np.asarray(w_down, f)[0].reshape(NFC, P, D).transpose(1, 0, 2)
    shared = {
        "w_in": np.concatenate([w_in_r[:, :, c0 + hh * 256:c0 + (hh + 1) * 256].reshape(P, -1) for c0 in W_IN_FM for hh in range(2)]
                               + [w_in_r[:, kh * 4:(kh + 1) * 4, W_IN_HI:W_IN_HI + 512].reshape(P, -1) for kh in range(2)]
                               + [w_in_r[:, :, W_IN_AKAV:W_IN_AKAV + 256].reshape(P, -1)], 1),
        "w_out": np.concatenate([w_out_r[:, kh * 4:(kh + 1) * 4, nh * 512:(nh + 1) * 512].reshape(P, -1) for nh in range(2) for kh in range(2)], 1),
        "w_gate": np.concatenate([w_gate_r[:, :, g * 256:(g + 1) * 256].reshape(P, -1) for g in range(11)], 1),
        "w_up": np.concatenate([w_up_r[:, :, g * 256:(g + 1) * 256].reshape(P, -1) for g in range(11)], 1),
        "w_down": np.concatenate([w_down_r[:, fg * 4:fg * 4 + a, nh * 512:(nh + 1) * 512].reshape(P, -1) for (nh, fg, a) in W_DOWN_SPECS], 1),
    }
    shared = {k: np.ascontiguousarray(v) for k, v in shared.items()}
    nw3 = np.stack([np.asarray(norm_mix_w, f)[0], np.asarray(norm_ffn_w, f)[0], np.asarray(final_norm_w, f)], 0)
    shared["nw3"] = np.ascontiguousarray(np.broadcast_to(nw3[None], (P, 3, D)))
    cst = np.zeros((P, NCST), f)
    lbl = np.asarray(lb_logits, f).reshape(2, 4, P)
    cst[:, C_LBL:C_LBL + 8] = lbl.transpose(2, 0, 1).reshape(P, 8)
    cst[:, C_HGW] = np.asarray(hg_norm_w, f)[0]
    cst[:, C_SINK:C_SINK + 8] = np.asarray(sinks, f)[0][None, :]
    ba = np.asarray(b_attn, f)[0]
    bqh = ba[0:512].reshape(8, 64)
    for j in range(4):
        cst[0:64, C_BQ + j] = bqh[j]
        cst[64:128, C_BQ + j] = bqh[4 + j]
    cst[:, C_BK] = ba[512:640]
    cst[:, C_CW:C_CW + 66] = np.asarray(conv_w, f)[0].reshape(3, NFC, P).transpose(2, 1, 0).reshape(P, 66)
    cst[:, C_CB:C_CB + NFC] = np.asarray(conv_b, f)[0].reshape(NFC, P).T
    cst[:, C_BV:C_BV + 128] = ba[640:768][None, :]
    shared["cst"] = cst
    return shared


def kernel(x, norm_mix_w, w_in, b_attn, lb_logits, hg_norm_w, sinks, w_out,
           norm_ffn_w, w_gate, w_up, conv_w, conv_b, w_down, final_norm_w):
    x = np.asarray(x, np.float32)
    B = x.shape[0]
    shared = prep_inputs(x, norm_mix_w, w_in, b_attn, lb_logits, hg_norm_w, sinks, w_out,
                         norm_ffn_w, w_gate, w_up, conv_w, conv_b, w_down, final_norm_w)
    nc = build_nc()
    in_maps = []
    for b in range(B):
        m = dict(shared)
        m["x"] = np.ascontiguousarray(x[b])
        in_maps.append(m)
    res = run_bass_kernel_spmd(nc, in_maps, core_ids=list(range(B)))
    return np.stack([np.asarray(r["out"], np.float32) for r in res.results], 0)
```
